# Optimizing a Trainium2 kernel written in Bass

```python
import math
import jax, jax.numpy as jnp
from jax import lax
import numpy as np

D_MODEL = 1024
BATCH = 16
SEQ = 2048
DEPTH = 1

MEM_LEN = 256
GRID_W = 64
BRANCH_W = 512
N_BRANCH = 3
SSM_GROUP = 16
SSM_GROUPS = BRANCH_W // SSM_GROUP
SSM_STATE = 64
NA_HEADS = 8
NA_HEAD_DIM = BRANCH_W // NA_HEADS
NA_WIN_R = 8
NA_WIN_C = 16
NA_QCB = 16
NA_KCB = NA_QCB + NA_WIN_C
MEM_HEADS = 4
MEM_HEAD_DIM = BRANCH_W // MEM_HEADS
D_FF = 2816
CONV_W = 3
DN_ALPHA = (2.0 * DEPTH) ** 0.25
DN_BETA = (8.0 * DEPTH) ** -0.25
LN_EPS = 1e-5
NEG_INF = -1e30
IN_COLS = 5 * BRANCH_W

kernel_name = 'hybrid_s5_natten_mem_convglu_deepnorm'


def layer_norm(x, g, b):
    xf = x.astype(jnp.float32)
    mu = jnp.mean(xf, -1, keepdims=True)
    var = jnp.mean(jnp.square(xf - mu), -1, keepdims=True)
    y = (xf - mu) * lax.rsqrt(var + LN_EPS)
    return (y * g.astype(jnp.float32) + b.astype(jnp.float32)).astype(x.dtype)


def _ssm_combine(e_i, e_j):
    a_i, b_i = e_i
    a_j, b_j = e_j
    return a_j * a_i, a_j * b_i + b_j


def s5_bidirectional(u, lam_re, lam_im, log_dt, b_re, b_im, c_re, c_im, d):
    bsz, seq, _ = u.shape
    f32 = jnp.float32
    uf = u.astype(f32).reshape(bsz, seq, SSM_GROUPS, SSM_GROUP)
    lam = lax.complex(lam_re.astype(f32), lam_im.astype(f32))
    step = jnp.exp(log_dt.astype(f32))[..., None]
    lam_bar = jnp.exp(lam * step)
    b_c = lax.complex(b_re.astype(f32), b_im.astype(f32))
    b_bar = ((lam_bar - 1.0) / lam)[..., None] * b_c
    c_c = lax.complex(c_re.astype(f32), c_im.astype(f32))
    y = d.astype(f32).reshape(SSM_GROUPS, SSM_GROUP) * uf
    for direction, rev in enumerate((False, True)):
        bu = jnp.einsum('bsgc,gpc->bsgp', uf, b_bar[direction])
        a = jnp.broadcast_to(lam_bar[direction][None, None], (1, seq, SSM_GROUPS, SSM_STATE))
        _, states = lax.associative_scan(_ssm_combine, (a, bu), reverse=rev, axis=1)
        y = y + jnp.einsum('gcp,bsgp->bsgc', c_c[direction], states).real
    return y.reshape(bsz, seq, BRANCH_W)


def neighborhood_attention(q, k, v, rpb):
    bsz, seq = q.shape[0], q.shape[1]
    rows = seq // GRID_W
    kr = min(NA_WIN_R, rows)
    ncb = GRID_W // NA_QCB
    r = jnp.arange(rows)
    key_rows = jnp.clip(r - kr // 2, 0, rows - kr)[:, None] + jnp.arange(kr)
    jb = jnp.arange(ncb)
    key_cols = jnp.clip(jb * NA_QCB - NA_WIN_C // 2, 0, GRID_W - NA_KCB)[:, None] + jnp.arange(NA_KCB)
    q_cols = jb[:, None] * NA_QCB + jnp.arange(NA_QCB)
    win_start = jnp.clip(q_cols - NA_WIN_C // 2, 0, GRID_W - NA_WIN_C)
    kc = key_cols[:, None, :]
    in_win = (kc >= win_start[..., None]) & (kc < win_start[..., None] + NA_WIN_C)
    dr = key_rows - r[:, None] + (NA_WIN_R - 1)
    dc = jnp.clip(kc - q_cols[..., None], -(NA_WIN_C - 1), NA_WIN_C - 1) + (NA_WIN_C - 1)
    bias = rpb.astype(jnp.float32)[:, dr[:, None, None, :, None], dc[None, :, :, None, :]]
    bias = jnp.where(in_win[None, None, :, :, None, :], bias, NEG_INF)
    bias = bias.reshape(NA_HEADS, rows, ncb, NA_QCB, kr * NA_KCB)
    idx = (key_rows[:, None, :, None] * GRID_W + key_cols[None, :, None, :]).reshape(-1)
    k_g = jnp.take(k, idx, axis=1).reshape(bsz, rows, ncb, kr * NA_KCB, NA_HEADS, NA_HEAD_DIM)
    v_g = jnp.take(v, idx, axis=1).reshape(bsz, rows, ncb, kr * NA_KCB, NA_HEADS, NA_HEAD_DIM)
    q_b = q.reshape(bsz, rows, ncb, NA_QCB, NA_HEADS, NA_HEAD_DIM)
    s = jnp.einsum('brjqhd,brjkhd->bhrjqk', q_b, k_g).astype(jnp.float32) * (NA_HEAD_DIM ** -0.5) + bias
    p = jax.nn.softmax(s, axis=-1).astype(v.dtype)
    o = jnp.einsum('bhrjqk,brjkhd->brjqhd', p, v_g)
    return o.reshape(bsz, seq, NA_HEADS * NA_HEAD_DIM)


def memory_attention(q, mem, w_mem_kv):
    bsz, mlen = mem.shape[0], mem.shape[1]
    kv = (mem @ w_mem_kv).reshape(bsz, mlen, 2, MEM_HEADS, MEM_HEAD_DIM)
    k, v = kv[:, :, 0], kv[:, :, 1]
    s = jnp.einsum('bshd,bmhd->bhsm', q, k).astype(jnp.float32) * (MEM_HEAD_DIM ** -0.5)
    p = jax.nn.softmax(s, axis=-1).astype(v.dtype)
    o = jnp.einsum('bhsm,bmhd->bshd', p, v)
    return o.reshape(q.shape[0], q.shape[1], MEM_HEADS * MEM_HEAD_DIM)


def depthwise_conv_seq(h, w, b):
    ch = h.shape[-1]
    out = lax.conv_general_dilated(
        h, w.reshape(CONV_W, 1, ch).astype(h.dtype), window_strides=(1,),
        padding=((CONV_W // 2, CONV_W // 2),), dimension_numbers=('NWC', 'WIO', 'NWC'),
        feature_group_count=ch)
    return out + b


def setup_inputs(seed: int = 0) -> dict:
    key = jax.random.key(seed)
    ks = jax.random.split(key, 30)
    f32 = jnp.float32
    nrm = lambda k, shape, s: (jax.random.normal(k, shape, f32) * s).astype(f32)
    G, P, C = SSM_GROUPS, SSM_STATE, SSM_GROUP
    lam_im0 = jnp.pi * jnp.arange(P, dtype=f32)
    return {
        'x': nrm(ks[0], (BATCH, SEQ, D_MODEL), 1.0),
        'mem': nrm(ks[1], (BATCH, MEM_LEN, D_MODEL), 1.0),
        'w_in': nrm(ks[2], (D_MODEL, IN_COLS), D_MODEL ** -0.5),
        'w_gate': nrm(ks[3], (D_MODEL, N_BRANCH * D_MODEL), D_MODEL ** -0.5),
        'b_gate': nrm(ks[4], (N_BRANCH * D_MODEL,), 0.01),
        'ssm_lambda_re': -0.5 + nrm(ks[5], (2, G, P), 0.01),
        'ssm_lambda_im': lam_im0 + nrm(ks[6], (2, G, P), 0.01),
        'ssm_log_dt': jax.random.uniform(ks[7], (2, G), f32, math.log(1e-3), math.log(1e-1)),
        'ssm_b_re': nrm(ks[8], (2, G, P, C), (2 * C) ** -0.5),
        'ssm_b_im': nrm(ks[9], (2, G, P, C), (2 * C) ** -0.5),
        'ssm_c_re': nrm(ks[10], (2, G, C, P), P ** -0.5),
        'ssm_c_im': nrm(ks[11], (2, G, C, P), P ** -0.5),
        'ssm_d': nrm(ks[12], (BRANCH_W,), 1.0),
        'w_glu': nrm(ks[13], (BRANCH_W, 2 * BRANCH_W), BRANCH_W ** -0.5),
        'b_glu': nrm(ks[14], (2 * BRANCH_W,), 0.01),
        'na_rpb': nrm(ks[15], (NA_HEADS, 2 * NA_WIN_R - 1, 2 * NA_WIN_C - 1), 0.02),
        'w_mem_kv': nrm(ks[16], (D_MODEL, 2 * BRANCH_W), D_MODEL ** -0.5),
        'w_branch': nrm(ks[17], (N_BRANCH, BRANCH_W, D_MODEL), BRANCH_W ** -0.5 * DN_BETA),
        'w_out': nrm(ks[18], (D_MODEL, D_MODEL), D_MODEL ** -0.5 * DN_BETA),
        'ln1_g': 1.0 + nrm(ks[19], (D_MODEL,), 0.02),
        'ln1_b': nrm(ks[20], (D_MODEL,), 0.02),
        'w_up': nrm(ks[21], (D_MODEL, 2 * D_FF), D_MODEL ** -0.5),
        'conv_w': nrm(ks[22], (CONV_W, 2 * D_FF), CONV_W ** -0.5),
        'conv_b': nrm(ks[23], (2 * D_FF,), 0.01),
        'w_down': nrm(ks[24], (D_FF, D_MODEL), D_FF ** -0.5 * DN_BETA),
        'ln2_g': 1.0 + nrm(ks[25], (D_MODEL,), 0.02),
        'ln2_b': nrm(ks[26], (D_MODEL,), 0.02),
    }


def reference(x, mem, w_in, w_gate, b_gate, ssm_lambda_re, ssm_lambda_im, ssm_log_dt,
              ssm_b_re, ssm_b_im, ssm_c_re, ssm_c_im, ssm_d, w_glu, b_glu, na_rpb,
              w_mem_kv, w_branch, w_out, ln1_g, ln1_b, w_up, conv_w, conv_b, w_down,
              ln2_g, ln2_b):
    bsz, seq, _ = x.shape
    for _layer in range(DEPTH):
        h = x @ w_in
        u_ssm, q_na, k_na, v_na, q_mem = jnp.split(
            h, [BRANCH_W, 2 * BRANCH_W, 3 * BRANCH_W, 4 * BRANCH_W], axis=-1)
        y = s5_bidirectional(u_ssm, ssm_lambda_re, ssm_lambda_im, ssm_log_dt,
                             ssm_b_re, ssm_b_im, ssm_c_re, ssm_c_im, ssm_d).astype(x.dtype)
        glu_a, glu_g = jnp.split(jax.nn.gelu(y) @ w_glu + b_glu, 2, axis=-1)
        br_ssm = glu_a * jax.nn.sigmoid(glu_g)
        hs = (bsz, seq, NA_HEADS, NA_HEAD_DIM)
        br_na = neighborhood_attention(q_na.reshape(hs), k_na.reshape(hs), v_na.reshape(hs), na_rpb)
        br_mem = memory_attention(q_mem.reshape(bsz, seq, MEM_HEADS, MEM_HEAD_DIM), mem, w_mem_kv)
        branches = jnp.stack([br_ssm, br_na, br_mem], axis=2)
        proj = jnp.einsum('bsnc,ncd->bsnd', branches, w_branch)
        gates = jax.nn.sigmoid((x @ w_gate + b_gate).reshape(bsz, seq, N_BRANCH, D_MODEL))
        mix = jnp.sum(gates * proj, axis=2) @ w_out
        x = layer_norm(DN_ALPHA * x + mix, ln1_g, ln1_b)
        hu = depthwise_conv_seq(x @ w_up, conv_w, conv_b)
        f_gate, f_val = jnp.split(hu, 2, axis=-1)
        ffn = (jax.nn.gelu(f_gate) * f_val) @ w_down
        x = layer_norm(DN_ALPHA * x + ffn, ln2_g, ln2_b)
    return x
```

```python
import math
import numpy as np
import concourse.bass as bass
import concourse.mybir as mybir
from concourse.bass_utils import run_bass_kernel_spmd
from contextlib import ExitStack

F32 = mybir.dt.float32
BF16 = mybir.dt.bfloat16
I32 = mybir.dt.int32
AF = mybir.ActivationFunctionType
ALU = mybir.AluOpType

NCORES = 8
D = 1024
SEQ = 2048
NSEQ = 2
TOK = NSEQ * SEQ
MEM = 256
DFF = 2816
ALPHA = 2.0 ** 0.25
EPS = 1e-5
NEG = -30000.0
MVALS = list(range(-7, 9)) + [16, 32, 64, 128, 256, 512, 1024]
NM = len(MVALS)


def MI(m):
    return MVALS.index(m)


class Sched:
    ENGS = ["sync", "scalar", "gpsimd", "vector", "tensor"]

    def __init__(self, nc, stack, n_dma_sems=40):
        self.nc = nc
        self.ops = {e: [] for e in self.ENGS}
        self.cnt = {e: 0 for e in self.ENGS}
        self.sem = {e: stack.enter_context(nc.semaphore("s_" + e)) for e in self.ENGS}
        self.dsem = [stack.enter_context(nc.semaphore("d_%d" % i)) for i in range(n_dma_sems)]
        self.dcnt = [0] * n_dma_sems
        self.dnext = 0
        self.last_w = {}
        self.readers = {}
        self.waited = {e: {} for e in self.ENGS}
        self.pending = {e: [] for e in self.ENGS}
        self.final_tokens = []

    def op(self, eng, fn, reads=(), writes=(), dma=False, final=False):
        deps = [(t, "bar") for t in self.pending[eng]]
        self.pending[eng] = []
        for r in reads:
            t = self.last_w.get(r)
            if t is not None:
                deps.append((t, "raw"))
        for w in writes:
            t = self.last_w.get(w)
            if t is not None:
                deps.append((t, "waw"))
            deps.extend((t, "war") for t in self.readers.get(w, ()))
        if dma:
            k = self.dnext
            self.dnext = (self.dnext + 1) % len(self.dsem)
            if self.dcnt[k] > 0:
                deps.append(((("d", k), self.dcnt[k] * 16, None), "dma"))
            self.dcnt[k] += 1
            tok = (("d", k), self.dcnt[k] * 16, None)
            inc = (self.dsem[k], 16)
        else:
            self.cnt[eng] += 1
            tok = (("e", eng), self.cnt[eng], eng)
            inc = (self.sem[eng], 1)
        need = {}
        for ((sk, val, seng), kind) in deps:
            if seng == eng and eng == "tensor":
                continue
            if seng == eng and not dma and eng in ("vector", "scalar") and kind in ("waw", "war") and False:
                continue
            if val <= self.waited[eng].get(sk, 0):
                continue
            if val > need.get(sk, 0):
                need[sk] = val
        waits = []
        for sk, val in need.items():
            self.waited[eng][sk] = val
            s = self.dsem[sk[1]] if sk[0] == "d" else self.sem[sk[1]]
            waits.append((s, val))
        self.ops[eng].append((fn, waits, inc))
        for r in reads:
            self.readers.setdefault(r, []).append(tok)
        for w in writes:
            self.last_w[w] = tok
            self.readers[w] = []
        if final:
            self.final_tokens.append(tok)
        return tok

    def barrier(self):
        toks = []
        for e in self.ENGS:
            if self.cnt[e] > 0:
                toks.append((("e", e), self.cnt[e], e))
        for k in range(len(self.dsem)):
            if self.dcnt[k] > 0:
                toks.append((("d", k), self.dcnt[k] * 16, None))
        for e in self.ENGS:
            self.pending[e] = list(toks)
        self.last_w = {}
        self.readers = {}

    def emit(self):
        nc = self.nc
        fw = []
        for (sk, val, _) in self.final_tokens:
            s = self.dsem[sk[1]] if sk[0] == "d" else self.sem[sk[1]]
            fw.append((s, val))
        with nc.Block() as block:
            def mk(ename):
                def body(eng):
                    for (fn, waits, inc) in self.ops[ename]:
                        for (s, v) in waits:
                            eng.wait_ge(s, v)
                        ins = fn(eng)
                        ins.then_inc(inc[0], inc[1])
                    if ename == "sync":
                        for (s, v) in fw:
                            eng.wait_ge(s, v)
                return body
            block.sync(mk("sync"))
            block.scalar(mk("scalar"))
            block.gpsimd(mk("gpsimd"))
            block.vector(mk("vector"))
            block.tensor(mk("tensor"))


class Arena:
    def __init__(self, base):
        self.base = base
        self.off = 0
        self.cap = base.shape[1] * 4
        self.peak = 0

    def alloc(self, n, dtype=F32):
        bpe = 2 if dtype == BF16 else 4
        off = (self.off + 31) // 32 * 32
        sz = n * bpe
        end = off + (sz + 3) // 4 * 4
        assert end <= self.cap, ("SBUF arena overflow", end, self.cap)
        self.off = end
        self.peak = max(self.peak, end)
        a = self.base[:, off // 4:end // 4]
        if dtype != F32:
            a = a.bitcast(dtype)[:, 0:n]
        return a

    def mark(self):
        return self.off

    def release(self, m):
        self.off = m


def na_patterns():
    def start(r):
        return min(max(r - 4, 0), 24)

    def ws(qc):
        return min(max(qc - 8, 0), 48)
    pats = {}
    pat_idx = []
    per_t = []
    qc = np.arange(64)
    kc = np.arange(64)
    colvalid = (kc[:, None] >= np.array([ws(q) for q in qc])[None, :]) & (kc[:, None] < np.array([ws(q) + 16 for q in qc])[None, :])
    dc = np.clip(kc[:, None] - qc[None, :], -15, 15) + 15
    for t in range(16):
        rows = [2 * t, 2 * t + 1]
        need = set()
        for r in rows:
            for kr in range(start(r), start(r) + 8):
                need.add(kr // 2)
        lst = []
        for kt in sorted(need):
            idx = np.full((128, 128), -1, np.int32)
            for a, kr in enumerate([2 * kt, 2 * kt + 1]):
                for b, r in enumerate(rows):
                    if start(r) <= kr < start(r) + 8:
                        dr = kr - r + 7
                        blk = np.where(colvalid, dr * 31 + dc, -1)
                        idx[a * 64:(a + 1) * 64, b * 64:(b + 1) * 64] = blk
            key = idx.tobytes()
            if key not in pats:
                pats[key] = len(pat_idx)
                pat_idx.append(idx)
            lst.append((kt, pats[key]))
        per_t.append(lst)
    return per_t, pat_idx


NA_PER_T, NA_PAT_IDX = na_patterns()
NPAT = len(NA_PAT_IDX)


def host_consts():
    ident = np.eye(128, dtype=np.float32)
    mv = np.broadcast_to(np.array(MVALS, np.float32)[None, :, None], (128, NM, 32)).reshape(128, NM * 32)
    masks = np.zeros((128, 6, 256), np.float32)
    for kt in range(2):
        for t4 in range(4):
            tl = 4 * kt + t4
            for g2 in range(2):
                for c in range(16):
                    r = t4 * 32 + g2 * 16 + c
                    for sl in range(8):
                        for g2b in range(2):
                            cols = slice(sl * 32 + g2b * 16, sl * 32 + g2b * 16 + 16)
                            if sl >= tl:
                                masks[r, kt, cols] = 1.0
                            if sl <= tl:
                                masks[r, 2 + kt, cols] = 1.0
                    masks[r, 4 + kt, tl * 32 + g2 * 16 + c] = 1.0 if True else 0.0
    cst = np.concatenate([ident, mv, masks.reshape(128, 6 * 256)], axis=1)
    return np.ascontiguousarray(cst, dtype=np.float32)


CST_IDENT = 0
CST_MV = 128
CST_MASK = 128 + NM * 32
CST_COLS = CST_MASK + 6 * 256


def host_ssm_params(lam_re, lam_im, log_dt, b_re, b_im, c_re, c_im, d):
    def p2(a):
        return a.reshape(2, 16, 2, 64).transpose(2, 3, 0, 1).reshape(128, 32)
    ldt = np.broadcast_to(log_dt.reshape(2, 16, 2).transpose(2, 0, 1)[:, None, :, :], (2, 64, 2, 16)).reshape(128, 32)

    def pb(a):
        return a.reshape(2, 16, 2, 64, 16).transpose(2, 3, 0, 1, 4).reshape(128, 512)

    def pc(a):
        return a.reshape(2, 16, 2, 16, 64).transpose(2, 4, 0, 1, 3).reshape(128, 512)
    dcol = np.broadcast_to(d.reshape(16, 2, 16).transpose(1, 2, 0)[None], (4, 2, 16, 16)).reshape(128, 16)
    sp = np.concatenate([p2(lam_re), p2(lam_im), ldt, pb(b_re), pb(b_im), pc(c_re), pc(c_im), dcol], axis=1)
    return np.ascontiguousarray(sp, dtype=np.float32)


SP_LRE, SP_LIM, SP_LDT, SP_BRE, SP_BIM, SP_CRE, SP_CIM, SP_DCOL = 0, 32, 64, 96, 608, 1120, 1632, 2144
SP_COLS = 2160


def host_small_params(b_gate, b_glu, ln1_g, ln1_b, ln2_g, ln2_b, conv_w, conv_b):
    def fm(v):
        return v.reshape(-1, 128).T
    cw = conv_w.reshape(3, 44, 128).transpose(2, 0, 1).reshape(128, 132)
    pp = np.concatenate([fm(b_gate), fm(b_glu), fm(ln1_g), fm(ln1_b), fm(ln2_g), fm(ln2_b), cw, fm(conv_b)], axis=1)
    return np.ascontiguousarray(pp, dtype=np.float32)


PP_BG, PP_BGLU, PP_L1G, PP_L1B, PP_L2G, PP_L2B, PP_CW, PP_CB = 0, 24, 32, 40, 48, 56, 64, 196
PP_COLS = 240


def host_na_bias(rpb):
    out = np.empty((128, NPAT, 8, 128), np.float32)
    flat = rpb.reshape(8, 15 * 31)
    for pi, idx in enumerate(NA_PAT_IDX):
        safe = np.where(idx >= 0, idx, 0)
        for h in range(8):
            out[:, pi, h, :] = np.where(idx >= 0, flat[h][safe], np.float32(NEG))
    return np.ascontiguousarray(out.reshape(128, NPAT * 8 * 128))


def build(debug=False, stop_after=None):
    nc = bass.Bass("TRN2", target_bir_lowering=False)

    def din(name, shape, dt=F32):
        return nc.dram_tensor(name, list(shape), dt, kind="ExternalInput").ap()

    xT = din("xT", [D, TOK])
    memT = din("memT", [D, NSEQ * MEM])
    w_in = din("w_in", [D, 2560])
    w_gate = din("w_gate", [D, 3072])
    w_glu = din("w_glu", [512, 1024])
    w_kv = din("w_mem_kv", [D, 1024])
    w_br = din("w_branch", [1536, D])
    w_out = din("w_out", [D, D])
    w_up = din("w_up", [D, 5632])
    w_down = din("w_down", [DFF, D])
    sp_in = din("sp", [128, SP_COLS])
    pp_in = din("pp", [128, PP_COLS])
    cst_in = din("cst", [128, CST_COLS])
    nab_in = din("nab", [128, NPAT * 8 * 128])
    yT = nc.dram_tensor("yT", [D, TOK], F32, kind="ExternalOutput").ap()
    dbg = {}
    if debug:
        for nm, shp in [("d_ssm", [512, TOK]), ("d_na", [512, TOK]), ("d_mem", [512, TOK]), ("d_x1", [D, TOK]), ("d_pwr", [128, NM * 32]), ("d_pwi", [128, NM * 32])]:
            dbg[nm] = nc.dram_tensor(nm, shp, F32, kind="ExternalOutput").ap()
        dbg["d_yg"] = nc.dram_tensor("d_yg", [128, 8192], F32, kind="ExternalOutput").ap()

    def dscr(name, shape, dt=BF16):
        return nc.dram_tensor(name, list(shape), dt, kind="Internal").ap()

    wb_in = dscr("wb_in", [D, 2560])
    wb_gate = dscr("wb_gate", [D, 3072])
    wb_glu = dscr("wb_glu", [512, 1024])
    wb_kv = dscr("wb_kv", [D, 1024])
    wb_br = dscr("wb_br", [1536, D])
    wb_out = dscr("wb_out", [D, D])
    wb_up = dscr("wb_up", [D, 5632])
    wb_down = dscr("wb_down", [DFF, D])
    brssm_h = dscr("brssm_h", [512, TOK])

    with ExitStack() as st:
        S = Sched(nc, st)
        arena_t = st.enter_context(nc.sbuf_tensor("arena", [128, 53100], F32))
        A = Arena(arena_t[:, :])
        pst = st.enter_context(nc.psum_tensor("ps", [128, 4096], F32))

        def bank(b, lo=0, hi=512):
            return pst[:, b * 512 + lo:b * 512 + hi]

        SY, AC, PO, VE, PE = "sync", "scalar", "gpsimd", "vector", "tensor"
        rr = {"i": 0}

        def evac_eng():
            rr["i"] += 1
            return VE if rr["i"] % 2 else AC

        def copy_op(eng, out, in_, reads, writes, scale=None):
            if eng == AC:
                if scale is None:
                    S.op(AC, lambda e: e.activation(out=out, in_=in_, func=AF.Copy), reads=reads, writes=writes)
                else:
                    S.op(AC, lambda e: e.activation(out=out, in_=in_, func=AF.Copy, scale=float(scale)), reads=reads, writes=writes)
            else:
                if scale is None:
                    S.op(eng, lambda e: e.tensor_copy(out=out, in_=in_), reads=reads, writes=writes)
                else:
                    S.op(eng, lambda e: e.tensor_scalar(out=out, in0=in_, scalar1=float(scale), scalar2=None, op0=ALU.mult), reads=reads, writes=writes)

        def dma(eng, out, in_, reads, writes, final=False, slow=False):
            if slow:
                return S.op(eng, lambda e: e.dma_start(out=out, in_=in_, allow_slow_non_contiguous=True), reads=reads, writes=writes, dma=True, final=final)
            return S.op(eng, lambda e: e.dma_start(out=out, in_=in_), reads=reads, writes=writes, dma=True, final=final)

        dcount = {"i": 0}

        def ddump(name, src, n, reads):
            if not debug:
                return
            outt = nc.dram_tensor(name, [128, n], F32, kind="ExternalOutput").ap()
            for c0 in range(0, n, 512):
                c1 = min(n, c0 + 512)
                stg = dcount["bufs"][dcount["i"] % 2]
                k = ("glua", dcount["i"] % 2)
                dcount["i"] += 1
                S.op(VE, lambda e, stg=stg, c0=c0, c1=c1: e.tensor_copy(out=stg[:, 0:c1 - c0], in_=src[:, c0:c1]), reads=reads, writes=[k])
                dma(SY, outt[:, c0:c1], stg[:, 0:c1 - c0], [k], [], final=True)

        def mm(e, out, lhsT, rhs, start, stop, tp=None):
            if tp is None:
                return e.matmul(out, lhsT=lhsT, rhs=rhs, start=start, stop=stop)
            return e.matmul(out, lhsT=lhsT, rhs=rhs, start=start, stop=stop, tile_position=tp)

        cst = A.alloc(CST_COLS)
        pp = A.alloc(PP_COLS)
        identf = cst[:, CST_IDENT:CST_IDENT + 128]
        identb = A.alloc(128, BF16)
        onesb = A.alloc(128, BF16)
        onesf = A.alloc(128)
        dma(SY, cst, cst_in[:, :], [], ["cst"])
        dma(SY, pp, pp_in[:, :], [], ["pp"])
        S.op(VE, lambda e: e.tensor_copy(out=identb, in_=identf), reads=["cst"], writes=["identb"])
        S.op(VE, lambda e: e.memset(onesb, 1.0), writes=["onesb"])
        S.op(VE, lambda e: e.memset(onesf, 1.0), writes=["onesf"])

        mW = A.mark()
        NWB = 4
        wst = [A.alloc(3072) for _ in range(NWB)]
        wsb = [A.alloc(3072, BF16) for _ in range(NWB)]
        wi = 0
        engs3 = [VE, PO, VE, PO]
        for (src, dst, K, N) in [(w_in, wb_in, D, 2560), (w_glu, wb_glu, 512, 1024)]:
            cw_ = N
            for kc in range(K // 128):
                for n0 in range(0, N, cw_):
                    i = wi % NWB
                    dma(SY, wst[i][:, 0:cw_], src[kc * 128:(kc + 1) * 128, n0:n0 + cw_], [], [("wst", i)])
                    copy_op(engs3[wi % 4], wsb[i][:, 0:cw_], wst[i][:, 0:cw_], [("wst", i)], [("wsb", i)])
                    dma(AC, dst[kc * 128:(kc + 1) * 128, n0:n0 + cw_], wsb[i][:, 0:cw_], [("wsb", i)], [])
                    wi += 1
        S.barrier()
        A.release(mW)
        W2C = 512
        mW0 = A.mark()
        w2f = [A.alloc(W2C) for _ in range(2)]
        w2b = [A.alloc(W2C, BF16) for _ in range(2)]
        mW = A.mark()
        w2tasks = []
        for (src, dst, K, N) in [(w_gate, wb_gate, D, 3072), (w_kv, wb_kv, D, 1024), (w_br, wb_br, 1536, D), (w_out, wb_out, D, D),
                                 (w_up, wb_up, D, 5632), (w_down, wb_down, DFF, D)]:
            for kc in range(K // 128):
                for n0 in range(0, N, W2C):
                    n1 = min(N, n0 + W2C)
                    w2tasks.append((src[kc * 128:(kc + 1) * 128, n0:n1], dst[kc * 128:(kc + 1) * 128, n0:n1], n1 - n0))
        w2s = {"i": 0, "pend": None}

        def w2(n):
            for _ in range(n):
                i = w2s["i"]
                if i < len(w2tasks):
                    src_, dst_, cw_ = w2tasks[i]
                    b_ = i % 2
                    dma(SY, w2f[b_][:, 0:cw_], src_, [], [("w2f", b_)])
                    copy_op(PO, w2b[b_][:, 0:cw_], w2f[b_][:, 0:cw_], [("w2f", b_)], [("w2b", b_)])
                    w2s["i"] += 1
                if w2s["pend"] is not None:
                    dst_p, cw_p, b_p = w2s["pend"]
                    dma(SY, dst_p, w2b[b_p][:, 0:cw_p], [("w2b", b_p)], [])
                    w2s["pend"] = None
                if i < len(w2tasks):
                    w2s["pend"] = (dst_, cw_, b_)

        def w2_flush():
            while w2s["i"] < len(w2tasks) or w2s["pend"] is not None:
                w2(1)
        if stop_after == "W":
            S.emit()
            return nc

        Tp = A.alloc(16 * 2 * 256, BF16).rearrange("p (g k s m) -> p g k s m", g=16, k=2, s=8)
        XVb = A.alloc(2 * 2 * 16 * 2 * 128, BF16).rearrange("p (d r g k m) -> p d r g k m", d=2, r=2, g=16, k=2)
        YCb = A.alloc(2 * 2 * 16 * 256, BF16).rearrange("p (d r g s m) -> p d r g s m", d=2, r=2, g=16, s=8)
        PWr = A.alloc(NM * 32).rearrange("p (m e) -> p m e", m=NM)
        PWi = A.alloc(NM * 32).rearrange("p (m e) -> p m e", m=NM)
        PWni = A.alloc(NM * 32).rearrange("p (m e) -> p m e", m=NM)
        mS0 = A.mark()
        sp = A.alloc(SP_COLS)
        dma(SY, sp, sp_in[:, :], [], ["sp"])
        mv = cst[:, CST_MV:CST_MV + NM * 32].rearrange("p (m e) -> p m e", m=NM)
        masks = cst[:, CST_MASK:CST_MASK + 6 * 256].rearrange("p (k c) -> p k c", k=6)

        def sm(n=32):
            return A.alloc(n)
        step_t, mg, ang = sm(), sm(), sm()
        lre = sp[:, SP_LRE:SP_LRE + 32]
        lim = sp[:, SP_LIM:SP_LIM + 32]
        S.op(AC, lambda e: e.activation(out=step_t, in_=sp[:, SP_LDT:SP_LDT + 32], func=AF.Exp), reads=["sp"], writes=["step"])
        S.op(VE, lambda e: e.tensor_tensor(out=mg, in0=lre, in1=step_t, op=ALU.mult), reads=["sp", "step"], writes=["mg"])
        S.op(VE, lambda e: e.tensor_tensor(out=ang, in0=lim, in1=step_t, op=ALU.mult), reads=["sp", "step"], writes=["ang"])
        NME = NM * 32
        bufA, bufB, bufD, bufE, bufF = [A.alloc(NME) for _ in range(5)]
        xang = yv = bufA
        xmag = magt = bufB
        kq = bufD
        sh = cs = bufE
        ch = sn = bufF
        ki = A.alloc(NME).bitcast(I32)

        def v3(a):
            return a.rearrange("p (m e) -> p m e", m=NM)

        def bc_e(a):
            return a.unsqueeze(1).to_broadcast([128, NM, 32])
        S.op(VE, lambda e: e.tensor_tensor(out=v3(xang), in0=mv, in1=bc_e(ang), op=ALU.mult), reads=["cst", "ang"], writes=["xang"])
        S.op(VE, lambda e: e.tensor_tensor(out=v3(xmag), in0=mv, in1=bc_e(mg), op=ALU.mult), reads=["cst", "mg"], writes=["xmag"])
        S.op(VE, lambda e: e.tensor_scalar(out=ki, in0=xang, scalar1=float(1.0 / (2 * math.pi)), scalar2=None, op0=ALU.mult), reads=["xang"], writes=["ki"])
        S.op(VE, lambda e: e.tensor_copy(out=kq, in_=ki), reads=["ki"], writes=["kq"])
        C1 = 6.28125
        C2 = 2 * math.pi - C1
        S.op(VE, lambda e: e.scalar_tensor_tensor(out=yv, in0=kq, scalar=-C1, in1=xang, op0=ALU.mult, op1=ALU.add), reads=["kq", "xang"], writes=["xang"])
        S.op(VE, lambda e: e.scalar_tensor_tensor(out=yv, in0=kq, scalar=-C2, in1=yv, op0=ALU.mult, op1=ALU.add), reads=["kq", "xang"], writes=["xang"])
        halfpi = A.alloc(1)
        S.op(VE, lambda e: e.memset(halfpi, math.pi / 2), writes=["halfpi"])
        sh4 = kq
        ch4 = ki.bitcast(F32)
        S.op(AC, lambda e: e.activation(out=sh4, in_=yv, func=AF.Sin, scale=0.25), reads=["xang", "kq"], writes=["kq"])
        S.op(AC, lambda e: e.activation(out=ch4, in_=yv, func=AF.Sin, scale=0.25, bias=halfpi), reads=["xang", "halfpi", "ki", "kq"], writes=["ki"])
        S.op(VE, lambda e: e.scalar_tensor_tensor(out=sh, in0=sh4, scalar=2.0, in1=ch4, op0=ALU.mult, op1=ALU.mult), reads=["kq", "ki"], writes=["sh"])
        S.op(VE, lambda e: e.scalar_tensor_tensor(out=ch, in0=sh4, scalar=-2.0, in1=sh4, op0=ALU.mult, op1=ALU.mult), reads=["kq"], writes=["ch"])
        S.op(VE, lambda e: e.tensor_scalar(out=ch, in0=ch, scalar1=1.0, scalar2=None, op0=ALU.add), reads=["ch"], writes=["ch"])
        S.op(AC, lambda e: e.activation(out=magt, in_=xmag, func=AF.Exp), reads=["xmag"], writes=["xmag"])
        S.op(VE, lambda e: e.scalar_tensor_tensor(out=sn, in0=sh, scalar=2.0, in1=ch, op0=ALU.mult, op1=ALU.mult), reads=["sh", "ch"], writes=["ch"])
        S.op(VE, lambda e: e.scalar_tensor_tensor(out=cs, in0=sh, scalar=-2.0, in1=sh, op0=ALU.mult, op1=ALU.mult), reads=["sh", "ch"], writes=["sh"])
        S.op(VE, lambda e: e.tensor_scalar(out=cs, in0=cs, scalar1=1.0, scalar2=None, op0=ALU.add), reads=["sh"], writes=["sh"])
        PWr2 = PWr.rearrange("p m e -> p (m e)")
        PWi2 = PWi.rearrange("p m e -> p (m e)")
        PWni2 = PWni.rearrange("p m e -> p (m e)")
        S.op(VE, lambda e: e.tensor_tensor(out=PWr2, in0=magt, in1=cs, op=ALU.mult), reads=["xmag", "sh"], writes=["PW"])
        S.op(VE, lambda e: e.tensor_tensor(out=PWi2, in0=magt, in1=sn, op=ALU.mult), reads=["xmag", "ch"], writes=["PWi"])
        S.op(VE, lambda e: e.tensor_scalar(out=PWni2, in0=PWi2, scalar1=-1.0, scalar2=None, op0=ALU.mult), reads=["PWi"], writes=["PWni"])
        nr, den, rden, t1, t2, kr_, ki_ = [sm() for _ in range(7)]
        lbr = PWr[:, MI(1), :]
        lbi = PWi[:, MI(1), :]
        S.op(VE, lambda e: e.tensor_scalar(out=nr, in0=lbr, scalar1=-1.0, scalar2=None, op0=ALU.add), reads=["PW"], writes=["nr"])
        S.op(VE, lambda e: e.tensor_tensor(out=den, in0=lre, in1=lre, op=ALU.mult), reads=["sp"], writes=["den"])
        S.op(VE, lambda e: e.tensor_tensor(out=t1, in0=lim, in1=lim, op=ALU.mult), reads=["sp"], writes=["t1"])
        S.op(VE, lambda e: e.tensor_tensor(out=den, in0=den, in1=t1, op=ALU.add), reads=["den", "t1"], writes=["den"])
        S.op(VE, lambda e: e.reciprocal(out=rden, in_=den), reads=["den"], writes=["rden"])
        S.op(VE, lambda e: e.tensor_tensor(out=t1, in0=nr, in1=lre, op=ALU.mult), reads=["nr", "sp", "rden"], writes=["t1"])
        S.op(VE, lambda e: e.tensor_tensor(out=t2, in0=lbi, in1=lim, op=ALU.mult), reads=["PWi", "sp"], writes=["t2"])
        S.op(VE, lambda e: e.tensor_tensor(out=t1, in0=t1, in1=t2, op=ALU.add), reads=["t1", "t2"], writes=["t1"])
        S.op(VE, lambda e: e.tensor_tensor(out=kr_, in0=t1, in1=rden, op=ALU.mult), reads=["t1", "rden"], writes=["kr"])
        S.op(VE, lambda e: e.tensor_tensor(out=t1, in0=lbi, in1=lre, op=ALU.mult), reads=["PWi", "sp", "kr"], writes=["t1"])
        S.op(VE, lambda e: e.tensor_tensor(out=t2, in0=nr, in1=lim, op=ALU.mult), reads=["nr", "sp"], writes=["t2"])
        S.op(VE, lambda e: e.tensor_tensor(out=t1, in0=t1, in1=t2, op=ALU.subtract), reads=["t1", "t2"], writes=["t1"])
        S.op(VE, lambda e: e.tensor_tensor(out=ki_, in0=t1, in1=rden, op=ALU.mult), reads=["t1", "rden"], writes=["ki_"])
        Bbr, Bbi, tb1 = A.alloc(512), A.alloc(512), A.alloc(512)

        def e16(a):
            return a.rearrange("p (e c) -> p e c", c=16)

        def bc_c(a):
            return a.unsqueeze(2).to_broadcast([128, 32, 16])
        bre = e16(sp[:, SP_BRE:SP_BRE + 512])
        bim = e16(sp[:, SP_BIM:SP_BIM + 512])
        cre = e16(sp[:, SP_CRE:SP_CRE + 512])
        cim = e16(sp[:, SP_CIM:SP_CIM + 512])
        S.op(VE, lambda e: e.tensor_tensor(out=e16(Bbr), in0=bre, in1=bc_c(kr_), op=ALU.mult), reads=["sp", "kr"], writes=["Bbr"])
        S.op(VE, lambda e: e.tensor_tensor(out=e16(tb1), in0=bim, in1=bc_c(ki_), op=ALU.mult), reads=["sp", "ki_"], writes=["tb1"])
        S.op(VE, lambda e: e.tensor_tensor(out=Bbr, in0=Bbr, in1=tb1, op=ALU.subtract), reads=["Bbr", "tb1"], writes=["Bbr"])
        S.op(VE, lambda e: e.tensor_tensor(out=e16(Bbi), in0=bim, in1=bc_c(kr_), op=ALU.mult), reads=["sp", "kr", "Bbr"], writes=["Bbi"])
        S.op(VE, lambda e: e.tensor_tensor(out=e16(tb1), in0=bre, in1=bc_c(ki_), op=ALU.mult), reads=["sp", "ki_", "Bbr"], writes=["tb1"])
        S.op(VE, lambda e: e.tensor_tensor(out=Bbi, in0=Bbi, in1=tb1, op=ALU.add), reads=["Bbi", "tb1"], writes=["Bbi"])

        def build_cplx(name, Ar, Ai, mfun, want_negim, gp0, tmp):
            Or = A.alloc(2048)
            Oi = A.alloc(2048)
            Cr, Ci, tA_, tB_ = tmp
            k_o = (name, "r")
            k_i = (name, "i")
            S.op(PO, lambda e: e.memset(Or, 0.0), writes=[k_o])
            S.op(PO, lambda e: e.memset(Oi, 0.0), writes=[k_i])

            def cview(t, d):
                return t.rearrange("p (d g j c) -> p d g j c", d=2, g=4, j=8)[:, d, :, :, :]

            def view(t, half, d):
                v = t.rearrange("p (d g j h c) -> p d g j h c", d=2, g=4, j=8, h=2)
                return v[half * 64:(half + 1) * 64, d, :, :, half, :]

            def aview(a, d):
                return a[:, d * 16 + gp0:d * 16 + gp0 + 4, :].unsqueeze(2).to_broadcast([128, 4, 8, 16])

            def pview(P, d):
                m0 = MI(mfun(d, 0))
                m1_ = MI(mfun(d, 1))
                if m1_ > m0:
                    pv = P[:, m0:m0 + 8, d * 16 + gp0:d * 16 + gp0 + 4].rearrange("p m e -> p e m")
                else:
                    pv = P[:, m0 - 7:m0 + 1, d * 16 + gp0:d * 16 + gp0 + 4].rearrange("p m e -> p e m")[:, :, ::-1]
                return pv.unsqueeze(3).to_broadcast([128, 4, 8, 16])
            rd = ["PW", "PWi", "Bbr", "Bbi", "sp"]
            for d in range(2):
                S.op(VE, lambda e, d=d: e.tensor_tensor(out=cview(Cr, d), in0=aview(Ar, d), in1=pview(PWr, d), op=ALU.mult), reads=rd, writes=[("cc", "Cr", d)])
                S.op(VE, lambda e, d=d: e.tensor_tensor(out=cview(tA_, d), in0=aview(Ai, d), in1=pview(PWi, d), op=ALU.mult), reads=rd, writes=[("cc", "tA", d)])
                S.op(VE, lambda e, d=d: e.tensor_tensor(out=cview(Cr, d), in0=cview(Cr, d), in1=cview(tA_, d), op=ALU.subtract), reads=[("cc", "tA", d), ("cc", "Cr", d)], writes=[("cc", "Cr", d)])
                S.op(VE, lambda e, d=d: e.tensor_tensor(out=cview(Ci, d), in0=aview(Ar, d), in1=pview(PWi, d), op=ALU.mult), reads=rd, writes=[("cc", "Ci", d)])
                S.op(VE, lambda e, d=d: e.tensor_tensor(out=cview(tB_, d), in0=aview(Ai, d), in1=pview(PWr, d), op=ALU.mult), reads=rd, writes=[("cc", "tB", d)])
                S.op(VE, lambda e, d=d: e.tensor_tensor(out=cview(Ci, d), in0=cview(Ci, d), in1=cview(tB_, d), op=ALU.add), reads=[("cc", "tB", d), ("cc", "Ci", d)], writes=[("cc", "Ci", d)])
                for half in range(2):
                    S.op(AC, lambda e, d=d, h=half: e.activation(out=view(Or, h, d), in_=cview(Cr, d)[h * 64:(h + 1) * 64], func=AF.Copy), reads=[("cc", "Cr", d)], writes=[k_o])
                    S.op(AC, lambda e, d=d, h=half: e.activation(out=view(Oi, h, d), in_=cview(Ci, d)[h * 64:(h + 1) * 64], func=AF.Copy, scale=(-1.0 if want_negim else 1.0)), reads=[("cc", "Ci", d)], writes=[k_i])
            return Or, Oi, k_o, k_i

        Bb3r, Bb3i = e16(Bbr), e16(Bbi)

        def v5(t):
            return t.rearrange("p (d g j m) -> p d g j m", d=2, g=4, j=8)
        tga = A.alloc(256)
        tgb = A.alloc(256)
        ctmp = [A.alloc(1024) for _ in range(4)]
        dcol = sp[:, SP_DCOL:SP_DCOL + 16]
        pb_ = 0
        mPass = A.mark()
        for gq in range(4):
            gp0 = 4 * gq
            A.release(mPass)
            Xr, Xni, kXr, kXi = build_cplx("X", Bb3r, Bb3i, lambda d, j: (-j if d == 0 else j), True, gp0, ctmp)
            Yr, Yi, kYr, kYi = build_cplx("Y", cre, cim, lambda d, j: (j if d == 0 else -j), False, gp0, ctmp)
            for g in range(4):
                gp = gp0 + g
                for kt in range(2):
                    bk = pb_ % 8
                    pb_ += 1

                    def tgen(e, g=g, kt=kt, bk=bk, Xr=Xr, Xni=Xni, Yr=Yr, Yi=Yi):
                        ins = None
                        for d in range(2):
                            o = bank(bk, d * 256, d * 256 + 256)
                            mm(e, o, v5(Xr)[:, d, g, 4 * kt:4 * kt + 4, :], v5(Yr)[:, d, g, :, :], True, False)
                            ins = mm(e, o, v5(Xni)[:, d, g, 4 * kt:4 * kt + 4, :], v5(Yi)[:, d, g, :, :], False, True)
                        return ins
                    S.op(PE, tgen, reads=[kXr, kXi, kYr, kYi], writes=[("ps", bk)])
                    w2(3)
                    S.op(VE, lambda e, bk=bk, kt=kt: e.tensor_tensor(out=tga, in0=bank(bk, 0, 256), in1=masks[:, kt, :], op=ALU.mult), reads=[("ps", bk), "cst"], writes=["tga"])
                    S.op(VE, lambda e, bk=bk, kt=kt: e.tensor_tensor(out=tgb, in0=bank(bk, 256, 512), in1=masks[:, 2 + kt, :], op=ALU.mult), reads=[("ps", bk), "cst"], writes=["tgb"])
                    S.op(VE, lambda e: e.tensor_tensor(out=tga, in0=tga, in1=tgb, op=ALU.add), reads=["tga", "tgb"], writes=["tga"])
                    S.op(VE, lambda e, gp=gp, kt=kt: e.scalar_tensor_tensor(out=Tp[:, gp, kt, :, :], in0=masks[:, 4 + kt, :].rearrange("p (s m) -> p s m", s=8), scalar=dcol[:, gp:gp + 1],
                                                                          in1=tga.rearrange("p (s m) -> p s m", s=8), op0=ALU.mult, op1=ALU.add), reads=["tga", "cst", "sp"], writes=["Tp"])
            A.release(mPass)
            S.op(PO, lambda e: e.engine_nop(), reads=[kXr, kXi, kYr, kYi], writes=[("X", "r"), ("X", "i"), ("Y", "r"), ("Y", "i"), ("XV", "r"), ("XV", "i"), ("YC", "r"), ("YC", "i")])
            XVr, XVi, kXVr, kXVi = build_cplx("XV", Bb3r, Bb3i, lambda d, j: (7 - j if d == 0 else j), False, gp0, ctmp)
            YCr, YCni, kYCr, kYCi = build_cplx("YC", cre, cim, lambda d, j: (j + 1 if d == 0 else 8 - j), True, gp0, ctmp)
            for d in range(2):
                for ri, (src, ksrc) in enumerate([(YCr, kYCr), (YCni, kYCi)]):
                    copy_op(AC, YCb[:, d, ri, gp0:gp0 + 4, :, :], v5(src)[:, d, :, :, :], [ksrc], ["YCb"])
            for d in range(2):
                for ri, (src, ksrc) in enumerate([(XVr, kXVr), (XVi, kXVi)]):
                    for g0 in range(0, 4, 2):
                        bk = pb_ % 8
                        pb_ += 1

                        def tr(e, d=d, src=src, g0=g0, bk=bk):
                            ins = None
                            for a in range(2):
                                for kt in range(2):
                                    ins = e.transpose(bank(bk, (a * 2 + kt) * 128, (a * 2 + kt) * 128 + 128),
                                                      v5(src)[:, d, g0 + a, 4 * kt:4 * kt + 4, :], identf)
                            return ins
                        S.op(PE, tr, reads=[ksrc, "cst"], writes=[("ps", bk)])
                        copy_op(evac_eng(), XVb[:, d, ri, gp0 + g0:gp0 + g0 + 2, :, :], bank(bk).rearrange("p (a k m) -> p a k m", a=2, k=2), [("ps", bk)], ["XVb"])
            S.op(PO, lambda e: e.engine_nop(), reads=[kXVr, kXVi, kYCr, kYCi], writes=[("X", "r"), ("X", "i"), ("Y", "r"), ("Y", "i"), ("XV", "r"), ("XV", "i"), ("YC", "r"), ("YC", "i")])
        S.barrier()
        if debug:
            dma(SY, dbg["d_pwr"][:, :], PWr.rearrange("p m e -> p (m e)"), [], [], final=True)
            dma(SY, dbg["d_pwi"][:, :], PWi.rearrange("p m e -> p (m e)"), [], [], final=True)
            S.barrier()
        A.release(mS0)
        if stop_after == "S0":
            S.emit()
            return nc

        m1 = A.mark()
        Ug = A.alloc(16 * 512, BF16).rearrange("p (g k n) -> p g k n", g=16, k=2)
        xTv = xT.rearrange("(k p) t -> p k t", p=128)
        m1a = A.mark()
        for seq in range(NSEQ):
            A.release(m1a)
            winu = A.alloc(8 * 512, BF16).rearrange("p (k n) -> p k n", k=8)
            dma(SY, winu, wb_in.rearrange("(k p) n -> p k n", p=128)[:, :, 0:512], [], ["winu"])
            xb = A.alloc(8 * SEQ, BF16).rearrange("p (k t) -> p k t", k=8)
            xst = [A.alloc(4 * 512).rearrange("p (k t) -> p k t", k=4) for _ in range(2)]
            ci = 0
            for tb in range(4):
                for kc0 in range(0, 8, 4):
                    i = ci % 2
                    ci += 1
                    dma(SY, xst[i], xTv[:, kc0:kc0 + 4, seq * SEQ + tb * 512: seq * SEQ + (tb + 1) * 512], [], [("xst", i)])
                    copy_op(evac_eng(), xb[:, kc0:kc0 + 4, tb * 512:(tb + 1) * 512], xst[i], [("xst", i)], ["xb"])
            for gp in range(16):
                ub = gp % 4

                def ugen(e, gp=gp, ub=ub, winu=winu, xb=xb):
                    ins = None
                    for tl in range(8):
                        kt, t4 = divmod(tl, 4)
                        for kc in range(8):
                            ins = mm(e, pst[32 * t4:32 * t4 + 32, ub * 512 + kt * 256: ub * 512 + kt * 256 + 256],
                                     winu[:, kc, 32 * gp:32 * gp + 32], xb[:, kc, tl::8], kc == 0, kc == 7, tp=(0, 32 * t4))
                    return ins
                S.op(PE, ugen, reads=["winu", "xb"], writes=[("ps", ub)])
                w2(3)
                copy_op(evac_eng(), Ug[:, gp, :, :], bank(ub).rearrange("p (k n) -> p k n", k=2), [("ps", ub)], [("Ug", gp)])
            S.barrier()
            A.release(m1a)
            wglu = A.alloc(4 * 1024, BF16).rearrange("p (k n) -> p k n", k=4)
            dma(SY, wglu, wb_glu.rearrange("(k p) n -> p k n", p=128), [], ["wglu"])
            ygT = A.alloc(4 * SEQ, BF16).rearrange("p (f t) -> p f t", f=4)
            scb = [[[[A.alloc(384) for _ in range(2)] for _ in range(2)] for _ in range(2)] for _ in range(2)]
            sct = [[[A.alloc(256) for _ in range(2)] for _ in range(2)] for _ in range(2)]
            Sb = [A.alloc(4 * 256, BF16).rearrange("p (d r n) -> p d r n", d=2, r=2) for _ in range(2)]
            glua = [A.alloc(512) for _ in range(2)]
            dcount["bufs"] = glua
            glug = [A.alloc(512) for _ in range(2)]
            brs_st = [A.alloc(512, BF16) for _ in range(2)]
            for s_ in range(2):
                for d in range(2):
                    for pp_ in range(2):
                        for ri in range(2):
                            S.op(PO, lambda e, b=scb[s_][d][pp_][ri]: e.memset(b, 0.0), writes=[("scb", s_, d, pp_, ri)])
            def Vpart(gpp):
                par = (gpp // 2) % 2
                w2(4)
                for ss in range(2):
                    gp = gpp + ss
                    vb = 2 * (1 - ss)

                    def vgen(e, gp=gp, vb=vb):
                        ins = None
                        for d in range(2):
                            for ri in range(2):
                                for kt in range(2):
                                    ins = mm(e, bank(vb + d, ri * 256, ri * 256 + 256), XVb[:, d, ri, gp, kt, :], Ug[:, gp, kt, :], kt == 0, kt == 1)
                        return ins
                    S.op(PE, vgen, reads=["XVb", ("Ug", gp)], writes=[("ps", vb), ("ps", vb + 1)])
                    for d in range(2):
                        for ri in range(2):
                            dst = scb[ss][d][par][ri][:, 128:384] if d == 0 else scb[ss][d][par][ri][:, 0:256]
                            copy_op(AC, dst, bank(vb + d, ri * 256, ri * 256 + 256), [("ps", vb + d)], [("scb", ss, d, par, ri)])

            def KSpart(gpp):
                cur = (gpp // 2) % 2
                for k in range(8):
                    sh_ = 1 << k
                    mi = MI(8 * sh_)
                    first, second = [], []
                    for ss in range(2):
                        gp = gpp + ss
                        for d in range(2):
                            ee = d * 16 + gp
                            ar = PWr[:, mi, ee:ee + 1]
                            ai = PWi[:, mi, ee:ee + 1]
                            nai = PWni[:, mi, ee:ee + 1]
                            srcb = scb[ss][d][cur]
                            dstb = scb[ss][d][1 - cur]
                            if d == 0:
                                dat = slice(128, 384)
                                shf = slice(128 - sh_, 384 - sh_)
                            else:
                                dat = slice(0, 256)
                                shf = slice(sh_, 256 + sh_)
                            kS = [("scb", ss, d, cur, 0), ("scb", ss, d, cur, 1)]
                            kD = [("scb", ss, d, 1 - cur, 0), ("scb", ss, d, 1 - cur, 1)]
                            tA, tB = sct[ss][d]
                            first.append((lambda e, srcb=srcb, ar=ar, shf=shf, dat=dat, tA=tA: e.scalar_tensor_tensor(out=tA, in0=srcb[0][:, shf], scalar=ar, in1=srcb[0][:, dat], op0=ALU.mult, op1=ALU.add), kS + ["PW"], [("sct", ss, d, 0)]))
                            first.append((lambda e, srcb=srcb, ar=ar, shf=shf, dat=dat, tB=tB: e.scalar_tensor_tensor(out=tB, in0=srcb[1][:, shf], scalar=ar, in1=srcb[1][:, dat], op0=ALU.mult, op1=ALU.add), kS, [("sct", ss, d, 1)]))
                            second.append((lambda e, srcb=srcb, dstb=dstb, nai=nai, shf=shf, dat=dat, tA=tA: e.scalar_tensor_tensor(out=dstb[0][:, dat], in0=srcb[1][:, shf], scalar=nai, in1=tA, op0=ALU.mult, op1=ALU.add), kS + [("sct", ss, d, 0)], [kD[0]]))
                            second.append((lambda e, srcb=srcb, dstb=dstb, ai=ai, shf=shf, dat=dat, tB=tB: e.scalar_tensor_tensor(out=dstb[1][:, dat], in0=srcb[0][:, shf], scalar=ai, in1=tB, op0=ALU.mult, op1=ALU.add), kS + [("sct", ss, d, 1)], [kD[1]]))
                    for (fn, rd, wr) in first + second:
                        S.op(VE, fn, reads=rd, writes=wr)
                    cur = 1 - cur

            def Tail(gpp):
                cur = (gpp // 2) % 2
                for ss in range(2):
                    gp = gpp + ss
                    for d in range(2):
                        for ri in range(2):
                            srcv = scb[ss][d][cur][ri][:, 127:383] if d == 0 else scb[ss][d][cur][ri][:, 1:257]
                            copy_op(AC, Sb[ss][:, d, ri, :], srcv, [("scb", ss, d, cur, ri)], [("Sb", ss, d, ri)])
                    q4 = gp % 4

                    def ygen(e, gp=gp, ss=ss, q4=q4):
                        ins = None
                        for sl in range(8):
                            o = pst[32 * q4:32 * q4 + 32, 2048 + sl * 256:2048 + sl * 256 + 256]
                            mm(e, o, Tp[:, gp, 0, sl, :], Ug[:, gp, 0, :], True, False, tp=(0, 32 * q4))
                            mm(e, o, Tp[:, gp, 1, sl, :], Ug[:, gp, 1, :], False, False, tp=(0, 32 * q4))
                            for d in range(2):
                                for ri in range(2):
                                    ins = mm(e, o, YCb[:, d, ri, gp, sl, :], Sb[ss][:, d, ri, :], False, (d == 1 and ri == 1), tp=(0, 32 * q4))
                        return ins
                    S.op(PE, ygen, reads=["Tp", "YCb", ("Ug", gp)] + [("Sb", ss, d, ri) for d in range(2) for ri in range(2)], writes=["psY"])
                    if q4 == 3:
                        fc = gp // 4
                        for sl in range(8):
                            S.op(AC, lambda e, fc=fc, sl=sl: e.activation(out=ygT[:, fc, sl::8], in_=pst[:, 2048 + sl * 256:2048 + sl * 256 + 256], func=AF.Gelu_apprx_tanh), reads=["psY"], writes=["ygT"])

            Vpart(0)
            for gpp in range(0, 16, 2):
                KSpart(gpp)
                if gpp + 2 < 16:
                    Vpart(gpp + 2)
                Tail(gpp)
            if debug and seq == 0:
                for fc in range(4):
                    for hh in range(4):
                        i2 = (fc * 4 + hh) % 2
                        S.op(VE, lambda e, fc=fc, hh=hh, i2=i2: e.tensor_copy(out=glua[i2], in_=ygT[:, fc, hh * 512:(hh + 1) * 512]), reads=["ygT"], writes=[("glua", i2)])
                        dma(SY, dbg["d_yg"][:, fc * 2048 + hh * 512: fc * 2048 + (hh + 1) * 512], glua[i2], [("glua", i2)], [], final=True)
            for tb in range(4):
                tsl = slice(tb * 512, (tb + 1) * 512)
                for j in range(4):
                    i2 = (tb * 4 + j) % 2

                    def glu_mm(e, j=j, tsl=tsl):
                        ins = None
                        for half, bk in ((0, 0), (1, 1)):
                            for kc in range(4):
                                ins = mm(e, bank(bk), wglu[:, kc, half * 512 + j * 128: half * 512 + j * 128 + 128], ygT[:, kc, tsl], kc == 0, kc == 3)
                        return ins
                    S.op(PE, glu_mm, reads=["wglu", "ygT"], writes=[("ps", 0), ("ps", 1)])
                    S.op(AC, lambda e, j=j, i2=i2: e.activation(out=glug[i2], in_=bank(1), func=AF.Sigmoid, bias=pp[:, PP_BGLU + 4 + j:PP_BGLU + 5 + j]), reads=[("ps", 1), "pp"], writes=[("glug", i2)])
                    S.op(VE, lambda e, j=j, i2=i2: e.scalar_tensor_tensor(out=brs_st[i2], in0=bank(0), scalar=pp[:, PP_BGLU + j:PP_BGLU + j + 1], in1=glug[i2], op0=ALU.add, op1=ALU.mult), reads=[("ps", 0), "pp", ("glug", i2)], writes=[("brs", i2)])
                    dma(SY, brssm_h[j * 128:(j + 1) * 128, seq * SEQ + tb * 512: seq * SEQ + (tb + 1) * 512], brs_st[i2], [("brs", i2)], [])
                    if debug:
                        S.op(VE, lambda e, i2=i2: e.tensor_copy(out=glua[i2], in_=brs_st[i2]), reads=[("brs", i2)], writes=[("glua", i2)])
                        dma(SY, dbg["d_ssm"][j * 128:(j + 1) * 128, seq * SEQ + tb * 512: seq * SEQ + (tb + 1) * 512], glua[i2], [("glua", i2)], [], final=True)
            S.barrier()
        w2_flush()
        S.barrier()
        A.release(mW0)
        if stop_after == "P1":
            S.emit()
            return nc

        brna = A.alloc(4 * SEQ, BF16).rearrange("p (f t) -> p f t", f=4)
        mSeq = A.mark()
        for seq in range(NSEQ):
            A.release(mSeq)
            nab = A.alloc(NPAT * 8 * 128, BF16).rearrange("p (a h q) -> p a h q", a=NPAT, h=8)
            xb = A.alloc(8 * SEQ, BF16).rearrange("p (k t) -> p k t", k=8)
            xst = [A.alloc(4 * 512).rearrange("p (k t) -> p k t", k=4) for _ in range(2)]
            ci = 0
            for c0 in range(0, NPAT * 8 * 128, 2048):
                c1 = min(c0 + 2048, NPAT * 8 * 128)
                i = ci % 2
                ci += 1
                nst = xst[i].rearrange("p k t -> p (k t)")
                dma(SY, nst[:, 0:c1 - c0], nab_in[:, c0:c1], [], [("xst", i)])
                copy_op(evac_eng(), nab.rearrange("p a h q -> p (a h q)")[:, c0:c1], nst[:, 0:c1 - c0], [("xst", i)], ["nab"])
            nabf = nab.rearrange("p a h q -> p (a h q)")
            for c0 in range(0, NPAT * 8 * 128, 4096):
                c1 = min(c0 + 4096, NPAT * 8 * 128)
                S.op(AC, lambda e, c0=c0, c1=c1: e.activation(out=nabf[:, c0:c1], in_=nabf[:, c0:c1], func=AF.Exp), reads=["nab"], writes=["nab"])
            for tb in range(4):
                for kc0 in range(0, 8, 4):
                    i = ci % 2
                    ci += 1
                    dma(SY, xst[i], xTv[:, kc0:kc0 + 4, seq * SEQ + tb * 512: seq * SEQ + (tb + 1) * 512], [], [("xst", i)])
                    copy_op(evac_eng(), xb[:, kc0:kc0 + 4, tb * 512:(tb + 1) * 512], xst[i], [("xst", i)], ["xb"])
            wq = A.alloc(8 * 512, BF16).rearrange("p (k n) -> p k n", k=8)
            qT = A.alloc(4 * SEQ, BF16).rearrange("p (f t) -> p f t", f=4)
            kT = A.alloc(4 * SEQ, BF16).rearrange("p (f t) -> p f t", f=4)
            Vt = A.alloc(16 * 512, BF16).rearrange("p (t n) -> p t n", t=16)
            pbk = 0
            for which, dst in ((1, qT), (2, kT)):
                dma(SY, wq, wb_in.rearrange("(k p) n -> p k n", p=128)[:, :, which * 512:(which + 1) * 512], [], ["wq"])
                for f in range(4):
                    for tb in range(4):
                        bk = pbk % 4
                        pbk += 1

                        def pmm(e, f=f, tb=tb, bk=bk):
                            ins = None
                            for kc in range(8):
                                ins = mm(e, bank(bk), wq[:, kc, f * 128:(f + 1) * 128], xb[:, kc, tb * 512:(tb + 1) * 512], kc == 0, kc == 7)
                            return ins
                        S.op(PE, pmm, reads=["wq", "xb"], writes=[("ps", bk)])
                        eng = evac_eng()
                        copy_op(eng, dst[:, f, tb * 512:(tb + 1) * 512], bank(bk), [("ps", bk)], ["qk%d" % which], scale=(0.125 if which == 1 else None))
            dma(SY, wq, wb_in.rearrange("(k p) n -> p k n", p=128)[:, :, 1536:2048], [], ["wq"])
            for tt in range(16):
                bk = pbk % 4
                pbk += 1

                def vmm(e, tt=tt, bk=bk):
                    ins = None
                    for kc in range(8):
                        ins = mm(e, bank(bk), xb[:, kc, tt * 128:(tt + 1) * 128], wq[:, kc, :], kc == 0, kc == 7)
                    return ins
                S.op(PE, vmm, reads=["wq", "xb"], writes=[("ps", bk)])
                eng = evac_eng()
                copy_op(eng, Vt[:, tt, :], bank(bk), [("ps", bk)], ["Vt"])
            Pm = [A.alloc(640, BF16) for _ in range(3)]
            rec = [A.alloc(128) for _ in range(2)]
            dna = [A.alloc(128) for _ in range(2)]
            ui = 0
            unitsA, unitsB = [], []
            for tt in range(16):
                lst = NA_PER_T[tt]
                nk = len(lst)
                for hc in range(4):
                    ob = 4 + ((tt * 4 + hc) % 2)
                    for hh in range(2):
                        h = 2 * hc + hh
                        sb = (0, 2, 6)[ui % 3]
                        pi_ = ui % 3
                        ui += 1
                        rows = slice(64 * hh, 64 * hh + 64)

                        def partA(lst=lst, nk=nk, h=h, hc=hc, rows=rows, sb=sb, tt=tt, pi_=pi_):
                            def smm(e):
                                ins = None
                                for j, (kt, pat) in enumerate(lst):
                                    o = pst[:, sb * 512 + j * 128: sb * 512 + j * 128 + 128]
                                    ins = mm(e, o, kT[rows, hc, kt * 128:(kt + 1) * 128], qT[rows, hc, tt * 128:(tt + 1) * 128], True, True)
                                return ins
                            S.op(PE, smm, reads=["qk1", "qk2"], writes=[("ps", sb), ("ps", sb + 1)])
                            n1 = min(nk, 4) * 128
                            S.op(AC, lambda e: e.activation(out=Pm[pi_][:, 0:n1], in_=pst[:, sb * 512: sb * 512 + n1], func=AF.Exp), reads=[("ps", sb)], writes=[("Pm", pi_)])
                            if nk > 4:
                                S.op(AC, lambda e: e.activation(out=Pm[pi_][:, 512:640], in_=pst[:, sb * 512 + 512: sb * 512 + 640], func=AF.Exp), reads=[("ps", sb + 1), ("Pm", pi_)], writes=[("Pm", pi_)])

                            def pmul(e):
                                ins = None
                                for j, (kt, pat) in enumerate(lst):
                                    ins = e.tensor_tensor(out=Pm[pi_][:, j * 128:(j + 1) * 128], in0=Pm[pi_][:, j * 128:(j + 1) * 128], in1=nab[:, pat, h, :], op=ALU.mult)
                                return ins
                            S.op(VE, pmul, reads=[("Pm", pi_), "nab"], writes=[("Pm", pi_)])

                        def partB(lst=lst, nk=nk, h=h, hh=hh, hc=hc, ob=ob, pi_=pi_, tt=tt):
                            def pvmm(e):
                                ins = None
                                for j, (kt, pat) in enumerate(lst):
                                    mm(e, pst[64 * hh:64 * hh + 64, ob * 512: ob * 512 + 128], Vt[:, kt, 64 * h:64 * h + 64], Pm[pi_][:, j * 128:(j + 1) * 128], j == 0, j == nk - 1, tp=(0, 64 * hh))
                                for j, (kt, pat) in enumerate(lst):
                                    ins = mm(e, pst[64 * hh:64 * hh + 64, ob * 512 + 128: ob * 512 + 256], onesb[:, 0:64], Pm[pi_][:, j * 128:(j + 1) * 128], j == 0, j == nk - 1, tp=(0, 64 * hh))
                                return ins
                            S.op(PE, pvmm, reads=["Vt", ("Pm", pi_), "onesb"], writes=[("ps", ob)])
                            if hh == 1:
                                oi = (tt * 4 + hc) % 2
                                S.op(VE, lambda e: e.reciprocal(out=rec[oi], in_=pst[:, ob * 512 + 128: ob * 512 + 256]), reads=[("ps", ob)], writes=[("rec", oi)])
                                S.op(VE, lambda e: e.tensor_tensor(out=brna[:, hc, tt * 128:(tt + 1) * 128], in0=pst[:, ob * 512: ob * 512 + 128], in1=rec[oi], op=ALU.mult), reads=[("ps", ob), ("rec", oi)], writes=["brna"])
                                if debug:
                                    S.op(VE, lambda e: e.tensor_copy(out=dna[oi], in_=brna[:, hc, tt * 128:(tt + 1) * 128]), reads=["brna"], writes=[("dna", oi)])
                                    dma(SY, dbg["d_na"][hc * 128:(hc + 1) * 128, seq * SEQ + tt * 128: seq * SEQ + (tt + 1) * 128], dna[oi], [("dna", oi)], [], final=True)
                        unitsA.append(partA)
                        unitsB.append(partB)
            LAG = 2
            for u in range(len(unitsA) + LAG):
                if u < len(unitsA):
                    unitsA[u]()
                if u - LAG >= 0:
                    unitsB[u - LAG]()
            S.barrier()
            A.release(mSeq)
            if stop_after == "P2":
                S.emit()
                return nc
            phase3(nc, S, A, pst, bank, seq, locals())
            S.barrier()
            if stop_after == "P3":
                S.emit()
                return nc
        S.emit()
    return nc


def phase3(nc, S, A, pst, bank, seq, env):
    SY, AC, PO, VE, PE = "sync", "scalar", "gpsimd", "vector", "tensor"
    debug = env["debug"]
    dbg = env["dbg"]
    pp = env["pp"]
    brna = env["brna"]
    onesb = env["onesb"]
    onesf = env["onesf"]
    mm = env["mm"]
    dma = env["dma"]
    copy_op = env["copy_op"]
    evac_eng = env["evac_eng"]
    xTv = env["xTv"]
    memT = env["memT"]
    yT = env["yT"]
    brssm_h = env["brssm_h"]
    wb_in, wb_gate, wb_br, wb_out, wb_up, wb_down, wb_kv = (env[k] for k in ["wb_in", "wb_gate", "wb_br", "wb_out", "wb_up", "wb_down", "wb_kv"])
    NB = 4
    t2f = A.alloc(8 * 512).rearrange("p (f t) -> p f t", f=8)
    mst = t2f.rearrange("p f t -> p (f t)")[:, 0:2048].rearrange("p (k m) -> p k m", k=8)
    memb = A.alloc(8 * 256, BF16).rearrange("p (k m) -> p k m", k=8)
    kmT = A.alloc(4 * 256, BF16).rearrange("p (h m) -> p h m", h=4)
    Vm = A.alloc(2 * 512, BF16).rearrange("p (t n) -> p t n", t=2)
    RS = 4608
    NR = 3
    ring = [A.alloc(RS, BF16) for _ in range(NR)]
    dma(SY, mst, memT.rearrange("(k p) m -> p k m", p=128)[:, :, seq * 256:(seq + 1) * 256], [], ["t2f"])
    copy_op(VE, memb, mst, ["t2f"], ["memb"])
    wkv = ring[0][:, 0:4096].rearrange("p (k n) -> p k n", k=8)
    for half in range(2):
        dma(SY, wkv, wb_kv.rearrange("(k p) n -> p k n", p=128)[:, :, half * 512:(half + 1) * 512], [], [("ring", 0)])
        if half == 0:
            for h in range(4):
                def kmm(e, h=h):
                    ins = None
                    for kc in range(8):
                        ins = mm(e, bank(h, 0, 256), wkv[:, kc, h * 128:(h + 1) * 128], memb[:, kc, :], kc == 0, kc == 7)
                    return ins
                S.op(PE, kmm, reads=[("ring", 0), "memb"], writes=[("ps", h)])
                copy_op(evac_eng(), kmT[:, h, :], bank(h, 0, 256), [("ps", h)], ["kmT"])
        else:
            for mt in range(2):
                def vmm2(e, mt=mt):
                    ins = None
                    for kc in range(8):
                        ins = mm(e, bank(4 + mt), memb[:, kc, mt * 128:(mt + 1) * 128], wkv[:, kc, :], kc == 0, kc == 7)
                    return ins
                S.op(PE, vmm2, reads=[("ring", 0), "memb"], writes=[("ps", 4 + mt)])
                copy_op(evac_eng(), Vm[:, mt, :], bank(4 + mt), [("ps", 4 + mt)], ["Vm"])

    xf = [A.alloc(512) for _ in range(2)]
    xstg = [A.alloc(512) for _ in range(2)]
    sq = [A.alloc(512) for _ in range(2)]
    st_mean, st_var, st_rstd, st_nmr, st_tmp = [A.alloc(512) for _ in range(5)]
    X1 = A.alloc(8 * 513).rearrange("p (f t) -> p f t", f=8)
    XB = A.alloc(8 * 514, BF16).rearrange("p (f t) -> p f t", f=8)
    dtmp = None
    xblk = A.alloc(8 * 512, BF16).rearrange("p (k t) -> p k t", k=8)
    qm = A.alloc(4 * 512, BF16).rearrange("p (h t) -> p h t", h=4)
    Pmem2 = [A.alloc(2 * 512, BF16).rearrange("p (m t) -> p m t", m=2) for _ in range(2)]
    brm = A.alloc(4 * 512, BF16).rearrange("p (h t) -> p h t", h=4)
    brs = A.alloc(4 * 512, BF16).rearrange("p (h t) -> p h t", h=4)
    gat = A.alloc(3 * 512)
    gatv = gat.rearrange("p (n t) -> p n t", n=3)
    prd = A.alloc(3 * 512).rearrange("p (n t) -> p n t", n=3)
    gsum = A.alloc(8 * 512, BF16).rearrange("p (f t) -> p f t", f=8)
    recm2 = [A.alloc(512) for _ in range(2)]
    hid = A.alloc(22 * 512, BF16).rearrange("p (f t) -> p f t", f=22)
    accs = [(A.alloc(512), A.alloc(512), A.alloc(512)) for _ in range(2)]
    if seq == 0:
        print("[arena] phase3 top", A.off, "cap", A.cap)
    S.op(PO, lambda e: e.memset(XB.rearrange("p f t -> p (f t)"), 0.0), writes=["xbh"] + [("xbd", f) for f in range(8)])
    S.op(PO, lambda e: e.memset(X1.rearrange("p f t -> p (f t)"), 0.0), writes=["x1h"] + [("x1d", f) for f in range(8)])

    lnq = []

    def ln_chunk_stats(tag, tsrc_f, f, N, lag=1):
        i = f % 2
        src = tsrc_f(f)
        S.op(PO, lambda e, i=i, src=src: e.tensor_tensor(out=sq[i][:, 0:N], in0=src, in1=src, op=ALU.mult), reads=[("t", tag, f)], writes=[("sq", i)])

        def stmm(e, f=f, i=i, src=src):
            mm(e, bank(6, 0, N), onesf, src, f == 0, f == 7)
            return mm(e, bank(7, 0, N), onesf, sq[i][:, 0:N], f == 0, f == 7)
        lnq.append((stmm, [("sq", i), "onesf", ("t", tag, f)]))
        while len(lnq) > (lag if f < 7 else 0):
            fn_, rd_ = lnq.pop(0)
            S.op(PE, fn_, reads=rd_, writes=[("ps", 6), ("ps", 7)])

    def ln_apply(tag, tsrc_f, N, gcol, bcol, post):
        S.op(AC, lambda e: e.activation(out=st_mean[:, 0:N], in_=bank(6, 0, N), func=AF.Copy, scale=1.0 / D), reads=[("ps", 6)], writes=["st_mean"])
        S.op(VE, lambda e: e.tensor_tensor(out=st_tmp[:, 0:N], in0=st_mean[:, 0:N], in1=st_mean[:, 0:N], op=ALU.mult), reads=["st_mean"], writes=["st_tmp"])
        S.op(VE, lambda e: e.scalar_tensor_tensor(out=st_var[:, 0:N], in0=bank(7, 0, N), scalar=1.0 / D, in1=st_tmp[:, 0:N], op0=ALU.mult, op1=ALU.subtract), reads=[("ps", 7), "st_tmp"], writes=["st_var"])
        S.op(VE, lambda e: e.tensor_scalar(out=st_var[:, 0:N], in0=st_var[:, 0:N], scalar1=float(EPS), scalar2=None, op0=ALU.add), reads=["st_var"], writes=["st_var"])
        S.op(AC, lambda e: e.activation(out=st_tmp[:, 0:N], in_=st_var[:, 0:N], func=AF.Sqrt), reads=["st_var", "st_tmp"], writes=["st_tmp"])
        S.op(VE, lambda e: e.reciprocal(out=st_rstd[:, 0:N], in_=st_tmp[:, 0:N]), reads=["st_tmp"], writes=["st_rstd"])
        S.op(VE, lambda e: e.scalar_tensor_tensor(out=st_nmr[:, 0:N], in0=st_mean[:, 0:N], scalar=-1.0, in1=st_rstd[:, 0:N], op0=ALU.mult, op1=ALU.mult), reads=["st_mean", "st_rstd"], writes=["st_nmr"])
        for f in range(8):
            kf = ("t", tag, f)
            dst = tsrc_f(f)
            S.op(VE, lambda e, f=f, dst=dst: e.scalar_tensor_tensor(out=dst, in0=dst, scalar=pp[:, gcol + f:gcol + f + 1], in1=st_rstd[:, 0:N], op0=ALU.mult, op1=ALU.mult), reads=[kf, "st_rstd", "pp"], writes=[kf])
            S.op(VE, lambda e, f=f, dst=dst: e.scalar_tensor_tensor(out=dst, in0=st_nmr[:, 0:N], scalar=pp[:, gcol + f:gcol + f + 1], in1=dst, op0=ALU.mult, op1=ALU.add), reads=[kf, "st_nmr", "pp"], writes=[kf])
            S.op(AC, lambda e, f=f, dst=dst: e.activation(out=dst, in_=dst, func=AF.Identity, bias=pp[:, bcol + f:bcol + f + 1]), reads=[kf, "pp"], writes=[kf])
            post(f, kf)

    pieces = []
    rstate = {"loaded": 0}

    def flush(upto):
        while rstate["loaded"] < min(upto, len(pieces)):
            i = rstate["loaded"]
            pieces[i](ring[i % NR], ("ring", i % NR))
            rstate["loaded"] += 1

    def mixer_steps(b):
        steps = []
        t0 = seq * SEQ + b * 512
        tl0 = b * 512

        def s_load(_s, _k):
            for kc in range(8):
                i = kc % 2
                dma(SY, xstg[i], xTv[:, kc, t0:t0 + 512], [], [("xstg", i)])
                copy_op(VE if kc % 2 else AC, xblk[:, kc, :], xstg[i], [("xstg", i)], [("xblk", kc)])
            dma(SY, brs, brssm_h.rearrange("(f p) t -> p f t", p=128)[:, :, t0:t0 + 512], [], ["brs"])
        steps.append((None, s_load))
        xk = [("xblk", kc) for kc in range(8)]

        def p_qm(slot, key):
            dma(SY, slot[:, 0:4096].rearrange("p (k n) -> p k n", k=8), wb_in.rearrange("(k p) n -> p k n", p=128)[:, :, 2048:2560], [], [key])

        def s_qm(slot, key):
            w = slot[:, 0:4096].rearrange("p (k n) -> p k n", k=8)
            for h in range(4):
                bk = h % 2

                def qmm(e, h=h, bk=bk):
                    ins = None
                    for kc in range(8):
                        ins = mm(e, bank(bk), w[:, kc, h * 128:(h + 1) * 128], xblk[:, kc, :], kc == 0, kc == 7)
                    return ins
                S.op(PE, qmm, reads=[key] + xk, writes=[("ps", bk)])
                copy_op(evac_eng(), qm[:, h, :], bank(bk), [("ps", bk)], [("qm", h)])
            for h in range(4):
                hb = h % 2
                sb0 = 2 + 2 * hb
                ob0 = 0 if hb == 0 else 4
                Pmem = Pmem2[hb]
                recm = recm2[hb]

                def sc(e, h=h, sb0=sb0):
                    ins = None
                    for mt in range(2):
                        ins = mm(e, bank(sb0 + mt), kmT[:, h, mt * 128:(mt + 1) * 128], qm[:, h, :], True, True)
                    return ins
                S.op(PE, sc, reads=["kmT", ("qm", h)], writes=[("ps", sb0), ("ps", sb0 + 1)])
                for mt in range(2):
                    S.op(AC, lambda e, mt=mt, sb0=sb0, Pmem=Pmem: e.activation(out=Pmem[:, mt, :], in_=bank(sb0 + mt), func=AF.Exp, scale=128.0 ** -0.5), reads=[("ps", sb0 + mt)], writes=[("Pmem", hb, mt)])

                def pv(e, h=h, ob0=ob0, Pmem=Pmem):
                    ins = None
                    for mt in range(2):
                        mm(e, bank(ob0), Vm[:, mt, h * 128:(h + 1) * 128], Pmem[:, mt, :], mt == 0, mt == 1)
                    for mt in range(2):
                        ins = mm(e, bank(ob0 + 1), onesb, Pmem[:, mt, :], mt == 0, mt == 1)
                    return ins
                S.op(PE, pv, reads=["Vm", ("Pmem", hb, 0), ("Pmem", hb, 1), "onesb"], writes=[("ps", ob0), ("ps", ob0 + 1)])
                S.op(VE, lambda e, ob0=ob0, recm=recm: e.reciprocal(out=recm, in_=bank(ob0 + 1)), reads=[("ps", ob0 + 1)], writes=[("recm", hb)])
                S.op(VE, lambda e, h=h, ob0=ob0, recm=recm: e.tensor_tensor(out=brm[:, h, :], in0=bank(ob0), in1=recm, op=ALU.mult), reads=[("ps", ob0), ("recm", hb)], writes=[("brm", h)])
        steps.append((p_qm, s_qm))

        for f in range(8):
            def p_gb(slot, key, f=f):
                wg = slot[:, 0:3072].rearrange("p (k n m) -> p k n m", k=8, n=3)
                gsrc = wb_gate.rearrange("(k p) n -> p k n", p=128)
                for n in range(3):
                    dma(SY, wg[:, :, n, :], gsrc[:, :, n * 1024 + f * 128:n * 1024 + (f + 1) * 128], [], [key])
                wbv = slot[:, 3072:4608].rearrange("p (nk m) -> p nk m", nk=12)
                dma(SY, wbv, wb_br.rearrange("(nk p) m -> p nk m", p=128)[:, :, f * 128:(f + 1) * 128], [], [key])

            def s_gb(slot, key, f=f):
                wg = slot[:, 0:3072].rearrange("p (k n m) -> p k n m", k=8, n=3)
                wbv = slot[:, 3072:4608].rearrange("p (nk m) -> p nk m", nk=12)
                srcs = [brs, brna[:, :, tl0:tl0 + 512], brm]
                skeys = [["brs"], ["brna"], [("brm", h) for h in range(4)]]
                for n in range(3):
                    u_ = (f * 3 + n) % 3
                    bg, bb = 2 * u_, 2 * u_ + 1

                    def gbmm(e, n=n, bg=bg, bb=bb):
                        ins = None
                        for kc in range(8):
                            ins = mm(e, bank(bg), wg[:, kc, n, :], xblk[:, kc, :], kc == 0, kc == 7)
                        for kc in range(4):
                            ins = mm(e, bank(bb), wbv[:, n * 4 + kc, :], srcs[n][:, kc, :], kc == 0, kc == 3)
                        return ins
                    S.op(PE, gbmm, reads=[key] + xk + skeys[n], writes=[("ps", bg), ("ps", bb)])
                    S.op(AC, lambda e, n=n, bg=bg: e.activation(out=gatv[:, n, :], in_=bank(bg), func=AF.Sigmoid, bias=pp[:, PP_BG + n * 8 + f:PP_BG + n * 8 + f + 1]), reads=[("ps", bg), "pp"], writes=[("gat", n)])
                    S.op(VE, lambda e, n=n, bb=bb: e.tensor_tensor(out=prd[:, n, :], in0=bank(bb), in1=gatv[:, n, :], op=ALU.mult), reads=[("ps", bb), ("gat", n)], writes=[("prd", n)])
                S.op(PO, lambda e: e.tensor_tensor(out=prd[:, 0, :], in0=prd[:, 0, :], in1=prd[:, 1, :], op=ALU.add), reads=[("prd", 0), ("prd", 1)], writes=[("prd", 0)])
                S.op(VE, lambda e: e.tensor_tensor(out=gsum[:, f, :], in0=prd[:, 0, :], in1=prd[:, 2, :], op=ALU.add), reads=[("prd", 0), ("prd", 2)], writes=[("gsum", f)])
            steps.append((p_gb, s_gb))

        for half in range(2):
            def p_wo(slot, key, half=half):
                dma(SY, slot[:, 0:4096].rearrange("p (k n) -> p k n", k=8), wb_out.rearrange("(k p) n -> p k n", p=128)[:, :, half * 512:(half + 1) * 512], [], [key])

            def s_wo(slot, key, half=half):
                w = slot[:, 0:4096].rearrange("p (k n) -> p k n", k=8)
                if half == 0:
                    S.op(PO, lambda e: e.tensor_copy(out=XB[:, :, 0:2], in_=XB[:, :, 512:514]), reads=[("xbd", f) for f in range(8)], writes=["xbh"])
                    S.op(PO, lambda e: e.tensor_copy(out=X1[:, :, 0:1], in_=X1[:, :, 512:513]), reads=[("x1d", f) for f in range(8)], writes=["x1h"])
                for j in range(4):
                    f = half * 4 + j
                    bk = f % 4
                    xi_ = f % 2
                    dma(AC, xf[xi_], xTv[:, f, t0:t0 + 512], [], [("xf", xi_)])

                    def omm(e, j=j, bk=bk):
                        ins = None
                        for kc in range(8):
                            ins = mm(e, bank(bk), w[:, kc, j * 128:(j + 1) * 128], gsum[:, kc, :], kc == 0, kc == 7)
                        return ins
                    S.op(PE, omm, reads=[key] + [("gsum", kc) for kc in range(8)], writes=[("ps", bk)])
                    S.op(VE, lambda e, f=f, bk=bk, xi_=xi_: e.scalar_tensor_tensor(out=X1[:, f, 1:513], in0=xf[xi_], scalar=float(ALPHA), in1=bank(bk), op0=ALU.mult, op1=ALU.add), reads=[("xf", xi_), ("ps", bk), "x1h"], writes=[("x1d", f), ("t", ("m", b), f)])
                    ln_chunk_stats(("m", b), lambda ff: X1[:, ff, 1:513], f, 512)
                if half == 1:
                    def post(f, kf):
                        S.op(AC, lambda e, f=f: e.activation(out=XB[:, f, 2:514], in_=X1[:, f, 1:513], func=AF.Copy), reads=[kf, "xbh"], writes=[("xbd", f), ("x1d", f)])
                    ln_apply(("m", b), lambda ff: X1[:, ff, 1:513], 512, PP_L1G, PP_L1B, post)
                    if debug:
                        dma(SY, dbg["d_x1"].rearrange("(f p) t -> p f t", p=128)[:, :, t0:t0 + 512], X1[:, :, 1:513], [("x1d", f) for f in range(8)], [], final=True)
            steps.append((p_wo, s_wo))
        return steps

    def ffn_steps(w, N, tok_override=None):
        steps = []
        tok0 = seq * SEQ + 512 * w - 1 if tok_override is None else seq * SEQ + tok_override
        xbk = ["xbh"] + [("xbd", f) for f in range(8)]
        for pr in range(0, 22, 2):
            def p_up(slot, key, pr=pr):
                wv = slot[:, 0:4096].rearrange("p (k a m) -> p k a m", k=8, a=4)
                up = wb_up.rearrange("(k p) n -> p k n", p=128)
                dma(SY, wv[:, :, 0:2, :], up[:, :, pr * 128:(pr + 2) * 128].rearrange("p k (a m) -> p k a m", a=2), [], [key])
                dma(SY, wv[:, :, 2:4, :], up[:, :, DFF + pr * 128:DFF + (pr + 2) * 128].rearrange("p k (a m) -> p k a m", a=2), [], [key])

            def s_up(slot, key, pr=pr):
                wv = slot[:, 0:4096].rearrange("p (k a m) -> p k a m", k=8, a=4)
                for a in range(2):
                    j = pr + a
                    bs = 3 * (j % 2)
                    accg, accv, gl = accs[j % 2]
                    sx = j % 2

                    def umm(e, a=a, bs=bs):
                        ins = None
                        for which in range(2):
                            for kc in range(8):
                                ins = mm(e, bank(bs + which, 0, N), wv[:, kc, which * 2 + a, :], XB[:, kc, 1:1 + N], kc == 0, kc == 7)
                        for which in range(2):
                            for kc in range(8):
                                ins = mm(e, bank(bs + 2, which * 2, which * 2 + 2), wv[:, kc, which * 2 + a, :], XB[:, kc, 0:2 + N:1 + N], kc == 0, kc == 7)
                        return ins
                    S.op(PE, umm, reads=[key] + xbk, writes=[("ps", bs), ("ps", bs + 1), ("ps", bs + 2)])
                    for which, acc in ((0, accg), (1, accv)):
                        jj = j + 22 * which
                        w0 = pp[:, PP_CW + 0 * 44 + jj:PP_CW + 0 * 44 + jj + 1]
                        w1 = pp[:, PP_CW + 1 * 44 + jj:PP_CW + 1 * 44 + jj + 1]
                        w2 = pp[:, PP_CW + 2 * 44 + jj:PP_CW + 2 * 44 + jj + 1]
                        cb = pp[:, PP_CB + jj:PP_CB + jj + 1]
                        ka = ("acc", sx, which)
                        pb = bs + which
                        S.op(AC, lambda e, acc=acc, pb=pb, w1=w1, cb=cb: e.activation(out=acc[:, 0:N], in_=bank(pb, 0, N), func=AF.Identity, scale=w1, bias=cb), reads=[("ps", pb), "pp"], writes=[ka])
                        if N > 1:
                            S.op(VE, lambda e, acc=acc, pb=pb, w0=w0: e.scalar_tensor_tensor(out=acc[:, 1:N], in0=bank(pb, 0, N - 1), scalar=w0, in1=acc[:, 1:N], op0=ALU.mult, op1=ALU.add), reads=[("ps", pb), ka, "pp"], writes=[ka])
                            S.op(VE, lambda e, acc=acc, pb=pb, w2=w2: e.scalar_tensor_tensor(out=acc[:, 0:N - 1], in0=bank(pb, 1, N), scalar=w2, in1=acc[:, 0:N - 1], op0=ALU.mult, op1=ALU.add), reads=[("ps", pb), ka, "pp"], writes=[ka])
                        S.op(VE, lambda e, acc=acc, which=which, w0=w0, bs=bs: e.scalar_tensor_tensor(out=acc[:, 0:1], in0=bank(bs + 2, which * 2, which * 2 + 1), scalar=w0, in1=acc[:, 0:1], op0=ALU.mult, op1=ALU.add), reads=[("ps", bs + 2), ka, "pp"], writes=[ka])
                        S.op(VE, lambda e, acc=acc, which=which, w2=w2, bs=bs: e.scalar_tensor_tensor(out=acc[:, N - 1:N], in0=bank(bs + 2, which * 2 + 1, which * 2 + 2), scalar=w2, in1=acc[:, N - 1:N], op0=ALU.mult, op1=ALU.add), reads=[("ps", bs + 2), ka, "pp"], writes=[ka])
                    S.op(AC, lambda e, accg=accg, gl=gl: e.activation(out=gl[:, 0:N], in_=accg[:, 0:N], func=AF.Gelu_apprx_tanh), reads=[("acc", sx, 0)], writes=[("gl", sx)])
                    S.op(PO, lambda e, j=j, gl=gl, accv=accv: e.tensor_tensor(out=hid[:, j, 0:N], in0=gl[:, 0:N], in1=accv[:, 0:N], op=ALU.mult), reads=[("gl", sx), ("acc", sx, 1)], writes=[("hid", j)])
            steps.append((p_up, s_up))
        for f in range(8):
            def p_dn(slot, key, f=f):
                dma(SY, slot[:, 0:2816].rearrange("p (k m) -> p k m", k=22), wb_down.rearrange("(k p) n -> p k n", p=128)[:, :, f * 128:(f + 1) * 128], [], [key])

            def s_dn(slot, key, f=f):
                wv = slot[:, 0:2816].rearrange("p (k m) -> p k m", k=22)
                bk = f % 6

                def dmm_a(e, bk=bk):
                    ins = None
                    for kc in range(18):
                        ins = mm(e, bank(bk, 0, N), wv[:, kc, :], hid[:, kc, 0:N], kc == 0, False)
                    return ins

                def dmm_b(e, bk=bk):
                    ins = None
                    for kc in range(18, 22):
                        ins = mm(e, bank(bk, 0, N), wv[:, kc, :], hid[:, kc, 0:N], False, kc == 21)
                    return ins
                S.op(PE, dmm_a, reads=[key] + [("hid", j) for j in range(18)], writes=[("ps", bk)])
                S.op(PE, dmm_b, reads=[key] + [("hid", j) for j in range(18, 22)], writes=[("ps", bk)])
                S.op(VE, lambda e, bk=bk: e.scalar_tensor_tensor(out=t2f[:, f, 0:N], in0=X1[:, f, 0:N], scalar=float(ALPHA), in1=bank(bk, 0, N), op0=ALU.mult, op1=ALU.add), reads=[("x1d", f), "x1h", ("ps", bk)], writes=[("t", ("f", w), f), "t2f"])
                ln_chunk_stats(("f", w), lambda ff: t2f[:, ff, 0:N], f, N)
                if f == 7:
                    c0 = 1 if w == 0 else 0

                    def post(ff, kf):
                        dma(AC, yT[ff * 128:(ff + 1) * 128, tok0 + c0:tok0 + N], t2f[:, ff, c0:N], [kf, "t2f"], [], final=True)
                    ln_apply(("f", w), lambda ff: t2f[:, ff, 0:N], N, PP_L2G, PP_L2B, post)
            steps.append((p_dn, s_dn))
        return steps

    def final_shift_steps():
        def s_fin(_s, _k):
            S.op(PO, lambda e: e.tensor_copy(out=XB[:, :, 0:2], in_=XB[:, :, 511:513]), reads=[("xbd", f) for f in range(8)], writes=["xbh"])
            S.op(PO, lambda e: e.tensor_copy(out=XB[:, :, 2:3], in_=XB[:, :, 513:514]), reads=["xbh"] + [("xbd", f) for f in range(8)], writes=[("xbd", f) for f in range(8)])
            S.op(PO, lambda e: e.memset(XB[:, :, 3:4], 0.0), reads=["xbh"], writes=[("xbd", f) for f in range(8)])
            S.op(PO, lambda e: e.tensor_copy(out=X1[:, :, 0:1], in_=X1[:, :, 511:512]), reads=[("x1d", f) for f in range(8)], writes=["x1h"])
            S.op(PO, lambda e: e.tensor_copy(out=X1[:, :, 1:2], in_=X1[:, :, 512:513]), reads=["x1h"] + [("x1d", f) for f in range(8)], writes=[("x1d", f) for f in range(8)])
        return [(None, s_fin)]

    M = [mixer_steps(b) for b in range(NB)]
    F = [ffn_steps(w, 512) for w in range(NB)] + [final_shift_steps() + ffn_steps(NB, 2, tok_override=SEQ - 2)]
    order = list(M[0][:-2]) + (M[1][:1] if NB > 1 else []) + list(M[0][-2:])
    for w in range(NB):
        nxt = M[w + 1] if w + 1 < NB else []
        early, late = nxt[1:-2], nxt[-2:]
        if w + 2 < NB:
            late = late[:0] + [M[w + 2][0]] + late
        head, rest = early[:3], early[3:]
        order += head
        fs = F[w]
        k = 0
        for i, stp in enumerate(fs):
            order.append(stp)
            while k < len(rest) and (k + 1) * len(fs) <= (i + 1) * len(rest):
                order.append(rest[k])
                k += 1
        order += rest[k:]
        order += late
    order += F[NB]
    plan = []
    for (pc, fn) in order:
        if pc is not None:
            pieces.append(pc)
            plan.append((len(pieces) - 1, fn))
        else:
            plan.append((None, fn))
    for (pi, fn) in plan:
        if pi is None:
            fn(None, None)
        else:
            flush(pi + NR - 1)
            fn(ring[pi % NR], ("ring", pi % NR))


_CACHE = {}


def kernel(**inputs):
    f32 = np.float32
    x = np.asarray(inputs["x"], f32)
    mem = np.asarray(inputs["mem"], f32)
    sp = host_ssm_params(*(np.asarray(inputs[k], f32) for k in ["ssm_lambda_re", "ssm_lambda_im", "ssm_log_dt", "ssm_b_re", "ssm_b_im", "ssm_c_re", "ssm_c_im", "ssm_d"]))
    pp = host_small_params(*(np.asarray(inputs[k], f32) for k in ["b_gate", "b_glu", "ln1_g", "ln1_b", "ln2_g", "ln2_b", "conv_w", "conv_b"]))
    cst = host_consts()
    nab = host_na_bias(np.asarray(inputs["na_rpb"], f32))
    shared = {
        "w_in": np.ascontiguousarray(inputs["w_in"], f32), "w_gate": np.ascontiguousarray(inputs["w_gate"], f32),
        "w_glu": np.ascontiguousarray(inputs["w_glu"], f32), "w_mem_kv": np.ascontiguousarray(inputs["w_mem_kv"], f32),
        "w_branch": np.ascontiguousarray(np.asarray(inputs["w_branch"], f32).reshape(1536, D)),
        "w_out": np.ascontiguousarray(inputs["w_out"], f32), "w_up": np.ascontiguousarray(inputs["w_up"], f32),
        "w_down": np.ascontiguousarray(inputs["w_down"], f32), "sp": sp, "pp": pp, "cst": cst, "nab": nab,
    }
    in_maps = []
    for c in range(NCORES):
        xs = x[NSEQ * c:NSEQ * (c + 1)].reshape(TOK, D)
        ms = mem[NSEQ * c:NSEQ * (c + 1)].reshape(NSEQ * MEM, D)
        m = dict(shared)
        m["xT"] = np.ascontiguousarray(xs.T)
        m["memT"] = np.ascontiguousarray(ms.T)
        in_maps.append(m)
    if "nc" not in _CACHE:
        _CACHE["nc"] = build(False)
    res = run_bass_kernel_spmd(_CACHE["nc"], in_maps, core_ids=list(range(NCORES)))
    out = np.empty((NCORES * NSEQ, SEQ, D), f32)
    for c in range(NCORES):
        yT = np.asarray(res.results[c]["yT"], f32)
        out[NSEQ * c:NSEQ * (c + 1)] = yT.T.reshape(NSEQ, SEQ, D)
    return out
```

```python
import math
import numpy as np
import concourse.bass as bass
import concourse.mybir as mybir
from concourse.bass_utils import run_bass_kernel_spmd
from contextlib import ExitStack

F32 = mybir.dt.float32
BF16 = mybir.dt.bfloat16
I32 = mybir.dt.int32
AF = mybir.ActivationFunctionType
ALU = mybir.AluOpType

NCORES = 8
D = 1024
SEQ = 2048
NSEQ = 2
TOK = NSEQ * SEQ
MEM = 256
DFF = 2816
ALPHA = 2.0 ** 0.25
EPS = 1e-5
NEG = -30000.0
MVALS = list(range(-7, 9)) + [16, 32, 64, 128, 256, 512, 1024]
NM = len(MVALS)


def MI(m):
    return MVALS.index(m)


class Sched:
    ENGS = ["sync", "scalar", "gpsimd", "vector", "tensor"]

    def __init__(self, nc, stack, n_dma_sems=40):
        self.nc = nc
        self.ops = {e: [] for e in self.ENGS}
        self.cnt = {e: 0 for e in self.ENGS}
        self.sem = {e: stack.enter_context(nc.semaphore("s_" + e)) for e in self.ENGS}
        self.dsem = [stack.enter_context(nc.semaphore("d_%d" % i)) for i in range(n_dma_sems)]
        self.dcnt = [0] * n_dma_sems
        self.dnext = 0
        self.last_w = {}
        self.readers = {}
        self.waited = {e: {} for e in self.ENGS}
        self.pending = {e: [] for e in self.ENGS}
        self.final_tokens = []

    def op(self, eng, fn, reads=(), writes=(), dma=False, final=False):
        deps = [(t, "bar") for t in self.pending[eng]]
        self.pending[eng] = []
        for r in reads:
            t = self.last_w.get(r)
            if t is not None:
                deps.append((t, "raw"))
        for w in writes:
            t = self.last_w.get(w)
            if t is not None:
                deps.append((t, "waw"))
            deps.extend((t, "war") for t in self.readers.get(w, ()))
        if dma:
            k = self.dnext
            self.dnext = (self.dnext + 1) % len(self.dsem)
            if self.dcnt[k] > 0:
                deps.append(((("d", k), self.dcnt[k] * 16, None), "dma"))
            self.dcnt[k] += 1
            tok = (("d", k), self.dcnt[k] * 16, None)
            inc = (self.dsem[k], 16)
        else:
            self.cnt[eng] += 1
            tok = (("e", eng), self.cnt[eng], eng)
            inc = (self.sem[eng], 1)
        need = {}
        for ((sk, val, seng), kind) in deps:
            if seng == eng and eng == "tensor":
                continue
            if seng == eng and not dma and eng in ("vector", "scalar") and kind in ("waw", "war") and False:
                continue
            if val <= self.waited[eng].get(sk, 0):
                continue
            if val > need.get(sk, 0):
                need[sk] = val
        waits = []
        for sk, val in need.items():
            self.waited[eng][sk] = val
            s = self.dsem[sk[1]] if sk[0] == "d" else self.sem[sk[1]]
            waits.append((s, val))
        self.ops[eng].append((fn, waits, inc))
        for r in reads:
            self.readers.setdefault(r, []).append(tok)
        for w in writes:
            self.last_w[w] = tok
            self.readers[w] = []
        if final:
            self.final_tokens.append(tok)
        return tok

    def barrier(self):
        toks = []
        for e in self.ENGS:
            if self.cnt[e] > 0:
                toks.append((("e", e), self.cnt[e], e))
        for k in range(len(self.dsem)):
            if self.dcnt[k] > 0:
                toks.append((("d", k), self.dcnt[k] * 16, None))
        for e in self.ENGS:
            self.pending[e] = list(toks)
        self.last_w = {}
        self.readers = {}

    def emit(self):
        nc = self.nc
        fw = []
        for (sk, val, _) in self.final_tokens:
            s = self.dsem[sk[1]] if sk[0] == "d" else self.sem[sk[1]]
            fw.append((s, val))
        with nc.Block() as block:
            def mk(ename):
                def body(eng):
                    for (fn, waits, inc) in self.ops[ename]:
                        for (s, v) in waits:
                            eng.wait_ge(s, v)
                        ins = fn(eng)
                        ins.then_inc(inc[0], inc[1])
                    if ename == "sync":
                        for (s, v) in fw:
                            eng.wait_ge(s, v)
                return body
            block.sync(mk("sync"))
            block.scalar(mk("scalar"))
            block.gpsimd(mk("gpsimd"))
            block.vector(mk("vector"))
            block.tensor(mk("tensor"))


class Arena:
    def __init__(self, base):
        self.base = base
        self.off = 0
        self.cap = base.shape[1] * 4
        self.peak = 0

    def alloc(self, n, dtype=F32):
        bpe = 2 if dtype == BF16 else 4
        off = (self.off + 31) // 32 * 32
        sz = n * bpe
        end = off + (sz + 3) // 4 * 4
        assert end <= self.cap, ("SBUF arena overflow", end, self.cap)
        self.off = end
        self.peak = max(self.peak, end)
        a = self.base[:, off // 4:end // 4]
        if dtype != F32:
            a = a.bitcast(dtype)[:, 0:n]
        return a

    def mark(self):
        return self.off

    def release(self, m):
        self.off = m


def na_patterns():
    def start(r):
        return min(max(r - 4, 0), 24)

    def ws(qc):
        return min(max(qc - 8, 0), 48)
    pats = {}
    pat_idx = []
    per_t = []
    qc = np.arange(64)
    kc = np.arange(64)
    colvalid = (kc[:, None] >= np.array([ws(q) for q in qc])[None, :]) & (kc[:, None] < np.array([ws(q) + 16 for q in qc])[None, :])
    dc = np.clip(kc[:, None] - qc[None, :], -15, 15) + 15
    for t in range(16):
        rows = [2 * t, 2 * t + 1]
        need = set()
        for r in rows:
            for kr in range(start(r), start(r) + 8):
                need.add(kr // 2)
        lst = []
        for kt in sorted(need):
            idx = np.full((128, 128), -1, np.int32)
            for a, kr in enumerate([2 * kt, 2 * kt + 1]):
                for b, r in enumerate(rows):
                    if start(r) <= kr < start(r) + 8:
                        dr = kr - r + 7
                        blk = np.where(colvalid, dr * 31 + dc, -1)
                        idx[a * 64:(a + 1) * 64, b * 64:(b + 1) * 64] = blk
            key = idx.tobytes()
            if key not in pats:
                pats[key] = len(pat_idx)
                pat_idx.append(idx)
            lst.append((kt, pats[key]))
        per_t.append(lst)
    return per_t, pat_idx


NA_PER_T, NA_PAT_IDX = na_patterns()
NPAT = len(NA_PAT_IDX)


def host_consts():
    ident = np.eye(128, dtype=np.float32)
    mv = np.broadcast_to(np.array(MVALS, np.float32)[None, :, None], (128, NM, 32)).reshape(128, NM * 32)
    masks = np.zeros((128, 6, 256), np.float32)
    for kt in range(2):
        for t4 in range(4):
            tl = 4 * kt + t4
            for g2 in range(2):
                for c in range(16):
                    r = t4 * 32 + g2 * 16 + c
                    for sl in range(8):
                        for g2b in range(2):
                            cols = slice(sl * 32 + g2b * 16, sl * 32 + g2b * 16 + 16)
                            if sl >= tl:
                                masks[r, kt, cols] = 1.0
                            if sl <= tl:
                                masks[r, 2 + kt, cols] = 1.0
                    masks[r, 4 + kt, tl * 32 + g2 * 16 + c] = 1.0 if True else 0.0
    cst = np.concatenate([ident, mv, masks.reshape(128, 6 * 256)], axis=1)
    return np.ascontiguousarray(cst, dtype=np.float32)


CST_IDENT = 0
CST_MV = 128
CST_MASK = 128 + NM * 32
CST_COLS = CST_MASK + 6 * 256


def host_ssm_params(lam_re, lam_im, log_dt, b_re, b_im, c_re, c_im, d):
    def p2(a):
        return a.reshape(2, 16, 2, 64).transpose(2, 3, 0, 1).reshape(128, 32)
    ldt = np.broadcast_to(log_dt.reshape(2, 16, 2).transpose(2, 0, 1)[:, None, :, :], (2, 64, 2, 16)).reshape(128, 32)

    def pb(a):
        return a.reshape(2, 16, 2, 64, 16).transpose(2, 3, 0, 1, 4).reshape(128, 512)

    def pc(a):
        return a.reshape(2, 16, 2, 16, 64).transpose(2, 4, 0, 1, 3).reshape(128, 512)
    dcol = np.broadcast_to(d.reshape(16, 2, 16).transpose(1, 2, 0)[None], (4, 2, 16, 16)).reshape(128, 16)
    sp = np.concatenate([p2(lam_re), p2(lam_im), ldt, pb(b_re), pb(b_im), pc(c_re), pc(c_im), dcol], axis=1)
    return np.ascontiguousarray(sp, dtype=np.float32)


SP_LRE, SP_LIM, SP_LDT, SP_BRE, SP_BIM, SP_CRE, SP_CIM, SP_DCOL = 0, 32, 64, 96, 608, 1120, 1632, 2144
SP_COLS = 2160


def host_small_params(b_gate, b_glu, ln1_g, ln1_b, ln2_g, ln2_b, conv_w, conv_b):
    def fm(v):
        return v.reshape(-1, 128).T
    cw = conv_w.reshape(3, 44, 128).transpose(2, 0, 1).reshape(128, 132)
    pp = np.concatenate([fm(b_gate), fm(b_glu), fm(ln1_g), fm(ln1_b), fm(ln2_g), fm(ln2_b), cw, fm(conv_b)], axis=1)
    return np.ascontiguousarray(pp, dtype=np.float32)


PP_BG, PP_BGLU, PP_L1G, PP_L1B, PP_L2G, PP_L2B, PP_CW, PP_CB = 0, 24, 32, 40, 48, 56, 64, 196
PP_COLS = 240


def host_na_bias(rpb):
    out = np.empty((128, NPAT, 8, 128), np.float32)
    flat = rpb.reshape(8, 15 * 31)
    for pi, idx in enumerate(NA_PAT_IDX):
        safe = np.where(idx >= 0, idx, 0)
        for h in range(8):
            out[:, pi, h, :] = np.where(idx >= 0, flat[h][safe], np.float32(NEG))
    return np.ascontiguousarray(out.reshape(128, NPAT * 8 * 128))


def build(debug=False, stop_after=None):
    nc = bass.Bass("TRN2", target_bir_lowering=False)

    def din(name, shape, dt=F32):
        return nc.dram_tensor(name, list(shape), dt, kind="ExternalInput").ap()

    xT = din("xT", [D, TOK])
    memT = din("memT", [D, NSEQ * MEM])
    w_in = din("w_in", [D, 2560])
    w_gate = din("w_gate", [D, 3072])
    w_glu = din("w_glu", [512, 1024])
    w_kv = din("w_mem_kv", [D, 1024])
    w_br = din("w_branch", [1536, D])
    w_out = din("w_out", [D, D])
    w_up = din("w_up", [D, 5632])
    w_down = din("w_down", [DFF, D])
    sp_in = din("sp", [128, SP_COLS])
    pp_in = din("pp", [128, PP_COLS])
    cst_in = din("cst", [128, CST_COLS])
    nab_in = din("nab", [128, NPAT * 8 * 128])
    yT = nc.dram_tensor("yT", [D, TOK], F32, kind="ExternalOutput").ap()
    dbg = {}
    if debug:
        for nm, shp in [("d_ssm", [512, TOK]), ("d_na", [512, TOK]), ("d_mem", [512, TOK]), ("d_x1", [D, TOK]), ("d_pwr", [128, NM * 32]), ("d_pwi", [128, NM * 32])]:
            dbg[nm] = nc.dram_tensor(nm, shp, F32, kind="ExternalOutput").ap()
        dbg["d_yg"] = nc.dram_tensor("d_yg", [128, 8192], F32, kind="ExternalOutput").ap()

    def dscr(name, shape, dt=BF16):
        return nc.dram_tensor(name, list(shape), dt, kind="Internal").ap()

    wb_in = dscr("wb_in", [D, 2560])
    wb_gate = dscr("wb_gate", [D, 3072])
    wb_glu = dscr("wb_glu", [512, 1024])
    wb_kv = dscr("wb_kv", [D, 1024])
    wb_br = dscr("wb_br", [1536, D])
    wb_out = dscr("wb_out", [D, D])
    wb_up = dscr("wb_up", [D, 5632])
    wb_down = dscr("wb_down", [DFF, D])
    brssm_h = dscr("brssm_h", [512, TOK])

    with ExitStack() as st:
        S = Sched(nc, st)
        arena_t = st.enter_context(nc.sbuf_tensor("arena", [128, 53100], F32))
        A = Arena(arena_t[:, :])
        pst = st.enter_context(nc.psum_tensor("ps", [128, 4096], F32))

        def bank(b, lo=0, hi=512):
            return pst[:, b * 512 + lo:b * 512 + hi]

        SY, AC, PO, VE, PE = "sync", "scalar", "gpsimd", "vector", "tensor"
        rr = {"i": 0}

        def evac_eng():
            rr["i"] += 1
            return VE if rr["i"] % 2 else AC

        def copy_op(eng, out, in_, reads, writes, scale=None):
            if eng == AC:
                if scale is None:
                    S.op(AC, lambda e: e.activation(out=out, in_=in_, func=AF.Copy), reads=reads, writes=writes)
                else:
                    S.op(AC, lambda e: e.activation(out=out, in_=in_, func=AF.Copy, scale=float(scale)), reads=reads, writes=writes)
            else:
                if scale is None:
                    S.op(eng, lambda e: e.tensor_copy(out=out, in_=in_), reads=reads, writes=writes)
                else:
                    S.op(eng, lambda e: e.tensor_scalar(out=out, in0=in_, scalar1=float(scale), scalar2=None, op0=ALU.mult), reads=reads, writes=writes)

        def dma(eng, out, in_, reads, writes, final=False, slow=False):
            if slow:
                return S.op(eng, lambda e: e.dma_start(out=out, in_=in_, allow_slow_non_contiguous=True), reads=reads, writes=writes, dma=True, final=final)
            return S.op(eng, lambda e: e.dma_start(out=out, in_=in_), reads=reads, writes=writes, dma=True, final=final)

        dcount = {"i": 0}

        def ddump(name, src, n, reads):
            if not debug:
                return
            outt = nc.dram_tensor(name, [128, n], F32, kind="ExternalOutput").ap()
            for c0 in range(0, n, 512):
                c1 = min(n, c0 + 512)
                stg = dcount["bufs"][dcount["i"] % 2]
                k = ("glua", dcount["i"] % 2)
                dcount["i"] += 1
                S.op(VE, lambda e, stg=stg, c0=c0, c1=c1: e.tensor_copy(out=stg[:, 0:c1 - c0], in_=src[:, c0:c1]), reads=reads, writes=[k])
                dma(SY, outt[:, c0:c1], stg[:, 0:c1 - c0], [k], [], final=True)

        def mm(e, out, lhsT, rhs, start, stop, tp=None):
            if tp is None:
                return e.matmul(out, lhsT=lhsT, rhs=rhs, start=start, stop=stop)
            return e.matmul(out, lhsT=lhsT, rhs=rhs, start=start, stop=stop, tile_position=tp)

        cst = A.alloc(CST_COLS)
        pp = A.alloc(PP_COLS)
        identf = cst[:, CST_IDENT:CST_IDENT + 128]
        identb = A.alloc(128, BF16)
        onesb = A.alloc(128, BF16)
        onesf = A.alloc(128)
        dma(SY, cst, cst_in[:, :], [], ["cst"])
        dma(SY, pp, pp_in[:, :], [], ["pp"])
        S.op(VE, lambda e: e.tensor_copy(out=identb, in_=identf), reads=["cst"], writes=["identb"])
        S.op(VE, lambda e: e.memset(onesb, 1.0), writes=["onesb"])
        S.op(VE, lambda e: e.memset(onesf, 1.0), writes=["onesf"])

        mW = A.mark()
        NWB = 4
        wst = [A.alloc(3072) for _ in range(NWB)]
        wsb = [A.alloc(3072, BF16) for _ in range(NWB)]
        wi = 0
        engs3 = [VE, PO, VE, PO]
        for (src, dst, K, N) in [(w_in, wb_in, D, 2560), (w_glu, wb_glu, 512, 1024)]:
            cw_ = N
            for kc in range(K // 128):
                for n0 in range(0, N, cw_):
                    i = wi % NWB
                    dma(SY, wst[i][:, 0:cw_], src[kc * 128:(kc + 1) * 128, n0:n0 + cw_], [], [("wst", i)])
                    copy_op(engs3[wi % 4], wsb[i][:, 0:cw_], wst[i][:, 0:cw_], [("wst", i)], [("wsb", i)])
                    dma(AC, dst[kc * 128:(kc + 1) * 128, n0:n0 + cw_], wsb[i][:, 0:cw_], [("wsb", i)], [])
                    wi += 1
        S.barrier()
        A.release(mW)
        W2C = 512
        mW0 = A.mark()
        w2f = [A.alloc(W2C) for _ in range(2)]
        w2b = [A.alloc(W2C, BF16) for _ in range(2)]
        mW = A.mark()
        w2tasks = []
        for (src, dst, K, N) in [(w_gate, wb_gate, D, 3072), (w_kv, wb_kv, D, 1024), (w_br, wb_br, 1536, D), (w_out, wb_out, D, D),
                                 (w_up, wb_up, D, 5632), (w_down, wb_down, DFF, D)]:
            for kc in range(K // 128):
                for n0 in range(0, N, W2C):
                    n1 = min(N, n0 + W2C)
                    w2tasks.append((src[kc * 128:(kc + 1) * 128, n0:n1], dst[kc * 128:(kc + 1) * 128, n0:n1], n1 - n0))
        w2s = {"i": 0, "pend": None}

        def w2(n):
            for _ in range(n):
                i = w2s["i"]
                if i < len(w2tasks):
                    src_, dst_, cw_ = w2tasks[i]
                    b_ = i % 2
                    dma(SY, w2f[b_][:, 0:cw_], src_, [], [("w2f", b_)])
                    copy_op(PO, w2b[b_][:, 0:cw_], w2f[b_][:, 0:cw_], [("w2f", b_)], [("w2b", b_)])
                    w2s["i"] += 1
                if w2s["pend"] is not None:
                    dst_p, cw_p, b_p = w2s["pend"]
                    dma(SY, dst_p, w2b[b_p][:, 0:cw_p], [("w2b", b_p)], [])
                    w2s["pend"] = None
                if i < len(w2tasks):
                    w2s["pend"] = (dst_, cw_, b_)

        def w2_flush():
            while w2s["i"] < len(w2tasks) or w2s["pend"] is not None:
                w2(1)
        if stop_after == "W":
            S.emit()
            return nc

        Tp = A.alloc(16 * 2 * 256, BF16).rearrange("p (g k s m) -> p g k s m", g=16, k=2, s=8)
        XVb = A.alloc(2 * 2 * 16 * 2 * 128, BF16).rearrange("p (d r g k m) -> p d r g k m", d=2, r=2, g=16, k=2)
        YCb = A.alloc(2 * 2 * 16 * 256, BF16).rearrange("p (d r g s m) -> p d r g s m", d=2, r=2, g=16, s=8)
        PWr = A.alloc(NM * 32).rearrange("p (m e) -> p m e", m=NM)
        PWi = A.alloc(NM * 32).rearrange("p (m e) -> p m e", m=NM)
        PWni = A.alloc(NM * 32).rearrange("p (m e) -> p m e", m=NM)
        mS0 = A.mark()
        sp = A.alloc(SP_COLS)
        dma(SY, sp, sp_in[:, :], [], ["sp"])
        mv = cst[:, CST_MV:CST_MV + NM * 32].rearrange("p (m e) -> p m e", m=NM)
        masks = cst[:, CST_MASK:CST_MASK + 6 * 256].rearrange("p (k c) -> p k c", k=6)

        def sm(n=32):
            return A.alloc(n)
        step_t, mg, ang = sm(), sm(), sm()
        lre = sp[:, SP_LRE:SP_LRE + 32]
        lim = sp[:, SP_LIM:SP_LIM + 32]
        S.op(AC, lambda e: e.activation(out=step_t, in_=sp[:, SP_LDT:SP_LDT + 32], func=AF.Exp), reads=["sp"], writes=["step"])
        S.op(VE, lambda e: e.tensor_tensor(out=mg, in0=lre, in1=step_t, op=ALU.mult), reads=["sp", "step"], writes=["mg"])
        S.op(VE, lambda e: e.tensor_tensor(out=ang, in0=lim, in1=step_t, op=ALU.mult), reads=["sp", "step"], writes=["ang"])
        NME = NM * 32
        bufA, bufB, bufD, bufE, bufF = [A.alloc(NME) for _ in range(5)]
        xang = yv = bufA
        xmag = magt = bufB
        kq = bufD
        sh = cs = bufE
        ch = sn = bufF
        ki = A.alloc(NME).bitcast(I32)

        def v3(a):
            return a.rearrange("p (m e) -> p m e", m=NM)

        def bc_e(a):
            return a.unsqueeze(1).to_broadcast([128, NM, 32])
        S.op(VE, lambda e: e.tensor_tensor(out=v3(xang), in0=mv, in1=bc_e(ang), op=ALU.mult), reads=["cst", "ang"], writes=["xang"])
        S.op(VE, lambda e: e.tensor_tensor(out=v3(xmag), in0=mv, in1=bc_e(mg), op=ALU.mult), reads=["cst", "mg"], writes=["xmag"])
        S.op(VE, lambda e: e.tensor_scalar(out=ki, in0=xang, scalar1=float(1.0 / (2 * math.pi)), scalar2=None, op0=ALU.mult), reads=["xang"], writes=["ki"])
        S.op(VE, lambda e: e.tensor_copy(out=kq, in_=ki), reads=["ki"], writes=["kq"])
        C1 = 6.28125
        C2 = 2 * math.pi - C1
        S.op(VE, lambda e: e.scalar_tensor_tensor(out=yv, in0=kq, scalar=-C1, in1=xang, op0=ALU.mult, op1=ALU.add), reads=["kq", "xang"], writes=["xang"])
        S.op(VE, lambda e: e.scalar_tensor_tensor(out=yv, in0=kq, scalar=-C2, in1=yv, op0=ALU.mult, op1=ALU.add), reads=["kq", "xang"], writes=["xang"])
        halfpi = A.alloc(1)
        S.op(VE, lambda e: e.memset(halfpi, math.pi / 2), writes=["halfpi"])
        sh4 = kq
        ch4 = ki.bitcast(F32)
        S.op(AC, lambda e: e.activation(out=sh4, in_=yv, func=AF.Sin, scale=0.25), reads=["xang", "kq"], writes=["kq"])
        S.op(AC, lambda e: e.activation(out=ch4, in_=yv, func=AF.Sin, scale=0.25, bias=halfpi), reads=["xang", "halfpi", "ki", "kq"], writes=["ki"])
        S.op(VE, lambda e: e.scalar_tensor_tensor(out=sh, in0=sh4, scalar=2.0, in1=ch4, op0=ALU.mult, op1=ALU.mult), reads=["kq", "ki"], writes=["sh"])
        S.op(VE, lambda e: e.scalar_tensor_tensor(out=ch, in0=sh4, scalar=-2.0, in1=sh4, op0=ALU.mult, op1=ALU.mult), reads=["kq"], writes=["ch"])
        S.op(VE, lambda e: e.tensor_scalar(out=ch, in0=ch, scalar1=1.0, scalar2=None, op0=ALU.add), reads=["ch"], writes=["ch"])
        S.op(AC, lambda e: e.activation(out=magt, in_=xmag, func=AF.Exp), reads=["xmag"], writes=["xmag"])
        S.op(VE, lambda e: e.scalar_tensor_tensor(out=sn, in0=sh, scalar=2.0, in1=ch, op0=ALU.mult, op1=ALU.mult), reads=["sh", "ch"], writes=["ch"])
        S.op(VE, lambda e: e.scalar_tensor_tensor(out=cs, in0=sh, scalar=-2.0, in1=sh, op0=ALU.mult, op1=ALU.mult), reads=["sh", "ch"], writes=["sh"])
        S.op(VE, lambda e: e.tensor_scalar(out=cs, in0=cs, scalar1=1.0, scalar2=None, op0=ALU.add), reads=["sh"], writes=["sh"])
        PWr2 = PWr.rearrange("p m e -> p (m e)")
        PWi2 = PWi.rearrange("p m e -> p (m e)")
        PWni2 = PWni.rearrange("p m e -> p (m e)")
        S.op(VE, lambda e: e.tensor_tensor(out=PWr2, in0=magt, in1=cs, op=ALU.mult), reads=["xmag", "sh"], writes=["PW"])
        S.op(VE, lambda e: e.tensor_tensor(out=PWi2, in0=magt, in1=sn, op=ALU.mult), reads=["xmag", "ch"], writes=["PWi"])
        S.op(VE, lambda e: e.tensor_scalar(out=PWni2, in0=PWi2, scalar1=-1.0, scalar2=None, op0=ALU.mult), reads=["PWi"], writes=["PWni"])
        nr, den, rden, t1, t2, kr_, ki_ = [sm() for _ in range(7)]
        lbr = PWr[:, MI(1), :]
        lbi = PWi[:, MI(1), :]
        S.op(VE, lambda e: e.tensor_scalar(out=nr, in0=lbr, scalar1=-1.0, scalar2=None, op0=ALU.add), reads=["PW"], writes=["nr"])
        S.op(VE, lambda e: e.tensor_tensor(out=den, in0=lre, in1=lre, op=ALU.mult), reads=["sp"], writes=["den"])
        S.op(VE, lambda e: e.tensor_tensor(out=t1, in0=lim, in1=lim, op=ALU.mult), reads=["sp"], writes=["t1"])
        S.op(VE, lambda e: e.tensor_tensor(out=den, in0=den, in1=t1, op=ALU.add), reads=["den", "t1"], writes=["den"])
        S.op(VE, lambda e: e.reciprocal(out=rden, in_=den), reads=["den"], writes=["rden"])
        S.op(VE, lambda e: e.tensor_tensor(out=t1, in0=nr, in1=lre, op=ALU.mult), reads=["nr", "sp", "rden"], writes=["t1"])
        S.op(VE, lambda e: e.tensor_tensor(out=t2, in0=lbi, in1=lim, op=ALU.mult), reads=["PWi", "sp"], writes=["t2"])
        S.op(VE, lambda e: e.tensor_tensor(out=t1, in0=t1, in1=t2, op=ALU.add), reads=["t1", "t2"], writes=["t1"])
        S.op(VE, lambda e: e.tensor_tensor(out=kr_, in0=t1, in1=rden, op=ALU.mult), reads=["t1", "rden"], writes=["kr"])
        S.op(VE, lambda e: e.tensor_tensor(out=t1, in0=lbi, in1=lre, op=ALU.mult), reads=["PWi", "sp", "kr"], writes=["t1"])
        S.op(VE, lambda e: e.tensor_tensor(out=t2, in0=nr, in1=lim, op=ALU.mult), reads=["nr", "sp"], writes=["t2"])
        S.op(VE, lambda e: e.tensor_tensor(out=t1, in0=t1, in1=t2, op=ALU.subtract), reads=["t1", "t2"], writes=["t1"])
        S.op(VE, lambda e: e.tensor_tensor(out=ki_, in0=t1, in1=rden, op=ALU.mult), reads=["t1", "rden"], writes=["ki_"])
        Bbr, Bbi, tb1 = A.alloc(512), A.alloc(512), A.alloc(512)

        def e16(a):
            return a.rearrange("p (e c) -> p e c", c=16)

        def bc_c(a):
            return a.unsqueeze(2).to_broadcast([128, 32, 16])
        bre = e16(sp[:, SP_BRE:SP_BRE + 512])
        bim = e16(sp[:, SP_BIM:SP_BIM + 512])
        cre = e16(sp[:, SP_CRE:SP_CRE + 512])
        cim = e16(sp[:, SP_CIM:SP_CIM + 512])
        S.op(VE, lambda e: e.tensor_tensor(out=e16(Bbr), in0=bre, in1=bc_c(kr_), op=ALU.mult), reads=["sp", "kr"], writes=["Bbr"])
        S.op(VE, lambda e: e.tensor_tensor(out=e16(tb1), in0=bim, in1=bc_c(ki_), op=ALU.mult), reads=["sp", "ki_"], writes=["tb1"])
        S.op(VE, lambda e: e.tensor_tensor(out=Bbr, in0=Bbr, in1=tb1, op=ALU.subtract), reads=["Bbr", "tb1"], writes=["Bbr"])
        S.op(VE, lambda e: e.tensor_tensor(out=e16(Bbi), in0=bim, in1=bc_c(kr_), op=ALU.mult), reads=["sp", "kr", "Bbr"], writes=["Bbi"])
        S.op(VE, lambda e: e.tensor_tensor(out=e16(tb1), in0=bre, in1=bc_c(ki_), op=ALU.mult), reads=["sp", "ki_", "Bbr"], writes=["tb1"])
        S.op(VE, lambda e: e.tensor_tensor(out=Bbi, in0=Bbi, in1=tb1, op=ALU.add), reads=["Bbi", "tb1"], writes=["Bbi"])

        def build_cplx(name, Ar, Ai, mfun, want_negim, gp0, tmp):
            Or = A.alloc(2048)
            Oi = A.alloc(2048)
            Cr, Ci, tA_, tB_ = tmp
            k_o = (name, "r")
            k_i = (name, "i")
            S.op(PO, lambda e: e.memset(Or, 0.0), writes=[k_o])
            S.op(PO, lambda e: e.memset(Oi, 0.0), writes=[k_i])

            def cview(t, d):
                return t.rearrange("p (d g j c) -> p d g j c", d=2, g=4, j=8)[:, d, :, :, :]

            def view(t, half, d):
                v = t.rearrange("p (d g j h c) -> p d g j h c", d=2, g=4, j=8, h=2)
                return v[half * 64:(half + 1) * 64, d, :, :, half, :]

            def aview(a, d):
                return a[:, d * 16 + gp0:d * 16 + gp0 + 4, :].unsqueeze(2).to_broadcast([128, 4, 8, 16])

            def pview(P, d):
                m0 = MI(mfun(d, 0))
                m1_ = MI(mfun(d, 1))
                if m1_ > m0:
                    pv = P[:, m0:m0 + 8, d * 16 + gp0:d * 16 + gp0 + 4].rearrange("p m e -> p e m")
                else:
                    pv = P[:, m0 - 7:m0 + 1, d * 16 + gp0:d * 16 + gp0 + 4].rearrange("p m e -> p e m")[:, :, ::-1]
                return pv.unsqueeze(3).to_broadcast([128, 4, 8, 16])
            rd = ["PW", "PWi", "Bbr", "Bbi", "sp"]
            for d in range(2):
                S.op(VE, lambda e, d=d: e.tensor_tensor(out=cview(Cr, d), in0=aview(Ar, d), in1=pview(PWr, d), op=ALU.mult), reads=rd, writes=[("cc", "Cr", d)])
                S.op(VE, lambda e, d=d: e.tensor_tensor(out=cview(tA_, d), in0=aview(Ai, d), in1=pview(PWi, d), op=ALU.mult), reads=rd, writes=[("cc", "tA", d)])
                S.op(VE, lambda e, d=d: e.tensor_tensor(out=cview(Cr, d), in0=cview(Cr, d), in1=cview(tA_, d), op=ALU.subtract), reads=[("cc", "tA", d), ("cc", "Cr", d)], writes=[("cc", "Cr", d)])
                S.op(VE, lambda e, d=d: e.tensor_tensor(out=cview(Ci, d), in0=aview(Ar, d), in1=pview(PWi, d), op=ALU.mult), reads=rd, writes=[("cc", "Ci", d)])
                S.op(VE, lambda e, d=d: e.tensor_tensor(out=cview(tB_, d), in0=aview(Ai, d), in1=pview(PWr, d), op=ALU.mult), reads=rd, writes=[("cc", "tB", d)])
                S.op(VE, lambda e, d=d: e.tensor_tensor(out=cview(Ci, d), in0=cview(Ci, d), in1=cview(tB_, d), op=ALU.add), reads=[("cc", "tB", d), ("cc", "Ci", d)], writes=[("cc", "Ci", d)])
                for half in range(2):
                    S.op(AC, lambda e, d=d, h=half: e.activation(out=view(Or, h, d), in_=cview(Cr, d)[h * 64:(h + 1) * 64], func=AF.Copy), reads=[("cc", "Cr", d)], writes=[k_o])
                    S.op(AC, lambda e, d=d, h=half: e.activation(out=view(Oi, h, d), in_=cview(Ci, d)[h * 64:(h + 1) * 64], func=AF.Copy, scale=(-1.0 if want_negim else 1.0)), reads=[("cc", "Ci", d)], writes=[k_i])
            return Or, Oi, k_o, k_i

        Bb3r, Bb3i = e16(Bbr), e16(Bbi)

        def v5(t):
            return t.rearrange("p (d g j m) -> p d g j m", d=2, g=4, j=8)
        tga = A.alloc(256)
        tgb = A.alloc(256)
        ctmp = [A.alloc(1024) for _ in range(4)]
        dcol = sp[:, SP_DCOL:SP_DCOL + 16]
        pb_ = 0
        mPass = A.mark()
        for gq in range(4):
            gp0 = 4 * gq
            A.release(mPass)
            Xr, Xni, kXr, kXi = build_cplx("X", Bb3r, Bb3i, lambda d, j: (-j if d == 0 else j), True, gp0, ctmp)
            Yr, Yi, kYr, kYi = build_cplx("Y", cre, cim, lambda d, j: (j if d == 0 else -j), False, gp0, ctmp)
            for g in range(4):
                gp = gp0 + g
                for kt in range(2):
                    bk = pb_ % 8
                    pb_ += 1

                    def tgen(e, g=g, kt=kt, bk=bk, Xr=Xr, Xni=Xni, Yr=Yr, Yi=Yi):
                        ins = None
                        for d in range(2):
                            o = bank(bk, d * 256, d * 256 + 256)
                            mm(e, o, v5(Xr)[:, d, g, 4 * kt:4 * kt + 4, :], v5(Yr)[:, d, g, :, :], True, False)
                            ins = mm(e, o, v5(Xni)[:, d, g, 4 * kt:4 * kt + 4, :], v5(Yi)[:, d, g, :, :], False, True)
                        return ins
                    S.op(PE, tgen, reads=[kXr, kXi, kYr, kYi], writes=[("ps", bk)])
                    w2(3)
                    S.op(VE, lambda e, bk=bk, kt=kt: e.tensor_tensor(out=tga, in0=bank(bk, 0, 256), in1=masks[:, kt, :], op=ALU.mult), reads=[("ps", bk), "cst"], writes=["tga"])
                    S.op(VE, lambda e, bk=bk, kt=kt: e.tensor_tensor(out=tgb, in0=bank(bk, 256, 512), in1=masks[:, 2 + kt, :], op=ALU.mult), reads=[("ps", bk), "cst"], writes=["tgb"])
                    S.op(VE, lambda e: e.tensor_tensor(out=tga, in0=tga, in1=tgb, op=ALU.add), reads=["tga", "tgb"], writes=["tga"])
                    S.op(VE, lambda e, gp=gp, kt=kt: e.scalar_tensor_tensor(out=Tp[:, gp, kt, :, :], in0=masks[:, 4 + kt, :].rearrange("p (s m) -> p s m", s=8), scalar=dcol[:, gp:gp + 1],
                                                                          in1=tga.rearrange("p (s m) -> p s m", s=8), op0=ALU.mult, op1=ALU.add), reads=["tga", "cst", "sp"], writes=["Tp"])
            A.release(mPass)
            S.op(PO, lambda e: e.engine_nop(), reads=[kXr, kXi, kYr, kYi], writes=[("X", "r"), ("X", "i"), ("Y", "r"), ("Y", "i"), ("XV", "r"), ("XV", "i"), ("YC", "r"), ("YC", "i")])
            XVr, XVi, kXVr, kXVi = build_cplx("XV", Bb3r, Bb3i, lambda d, j: (7 - j if d == 0 else j), False, gp0, ctmp)
            YCr, YCni, kYCr, kYCi = build_cplx("YC", cre, cim, lambda d, j: (j + 1 if d == 0 else 8 - j), True, gp0, ctmp)
            for d in range(2):
                for ri, (src, ksrc) in enumerate([(YCr, kYCr), (YCni, kYCi)]):
                    copy_op(AC, YCb[:, d, ri, gp0:gp0 + 4, :, :], v5(src)[:, d, :, :, :], [ksrc], ["YCb"])
            for d in range(2):
                for ri, (src, ksrc) in enumerate([(XVr, kXVr), (XVi, kXVi)]):
                    for g0 in range(0, 4, 2):
                        bk = pb_ % 8
                        pb_ += 1

                        def tr(e, d=d, src=src, g0=g0, bk=bk):
                            ins = None
                            for a in range(2):
                                for kt in range(2):
                                    ins = e.transpose(bank(bk, (a * 2 + kt) * 128, (a * 2 + kt) * 128 + 128),
                                                      v5(src)[:, d, g0 + a, 4 * kt:4 * kt + 4, :], identf)
                            return ins
                        S.op(PE, tr, reads=[ksrc, "cst"], writes=[("ps", bk)])
                        copy_op(evac_eng(), XVb[:, d, ri, gp0 + g0:gp0 + g0 + 2, :, :], bank(bk).rearrange("p (a k m) -> p a k m", a=2, k=2), [("ps", bk)], ["XVb"])
            S.op(PO, lambda e: e.engine_nop(), reads=[kXVr, kXVi, kYCr, kYCi], writes=[("X", "r"), ("X", "i"), ("Y", "r"), ("Y", "i"), ("XV", "r"), ("XV", "i"), ("YC", "r"), ("YC", "i")])
        S.barrier()
        if debug:
            dma(SY, dbg["d_pwr"][:, :], PWr.rearrange("p m e -> p (m e)"), [], [], final=True)
            dma(SY, dbg["d_pwi"][:, :], PWi.rearrange("p m e -> p (m e)"), [], [], final=True)
            S.barrier()
        A.release(mS0)
        if stop_after == "S0":
            S.emit()
            return nc

        m1 = A.mark()
        Ug = A.alloc(16 * 512, BF16).rearrange("p (g k n) -> p g k n", g=16, k=2)
        xTv = xT.rearrange("(k p) t -> p k t", p=128)
        m1a = A.mark()
        for seq in range(NSEQ):
            A.release(m1a)
            winu = A.alloc(8 * 512, BF16).rearrange("p (k n) -> p k n", k=8)
            dma(SY, winu, wb_in.rearrange("(k p) n -> p k n", p=128)[:, :, 0:512], [], ["winu"])
            xb = A.alloc(8 * SEQ, BF16).rearrange("p (k t) -> p k t", k=8)
            xst = [A.alloc(4 * 512).rearrange("p (k t) -> p k t", k=4) for _ in range(2)]
            ci = 0
            for tb in range(4):
                for kc0 in range(0, 8, 4):
                    i = ci % 2
                    ci += 1
                    dma(SY, xst[i], xTv[:, kc0:kc0 + 4, seq * SEQ + tb * 512: seq * SEQ + (tb + 1) * 512], [], [("xst", i)])
                    copy_op(evac_eng(), xb[:, kc0:kc0 + 4, tb * 512:(tb + 1) * 512], xst[i], [("xst", i)], ["xb"])
            for gp in range(16):
                ub = gp % 4

                def ugen(e, gp=gp, ub=ub, winu=winu, xb=xb):
                    ins = None
                    for tl in range(8):
                        kt, t4 = divmod(tl, 4)
                        for kc in range(8):
                            ins = mm(e, pst[32 * t4:32 * t4 + 32, ub * 512 + kt * 256: ub * 512 + kt * 256 + 256],
                                     winu[:, kc, 32 * gp:32 * gp + 32], xb[:, kc, tl::8], kc == 0, kc == 7, tp=(0, 32 * t4))
                    return ins
                S.op(PE, ugen, reads=["winu", "xb"], writes=[("ps", ub)])
                w2(3)
                copy_op(evac_eng(), Ug[:, gp, :, :], bank(ub).rearrange("p (k n) -> p k n", k=2), [("ps", ub)], [("Ug", gp)])
            S.barrier()
            A.release(m1a)
            wglu = A.alloc(4 * 1024, BF16).rearrange("p (k n) -> p k n", k=4)
            dma(SY, wglu, wb_glu.rearrange("(k p) n -> p k n", p=128), [], ["wglu"])
            ygT = A.alloc(4 * SEQ, BF16).rearrange("p (f t) -> p f t", f=4)
            scb = [[[[A.alloc(384) for _ in range(2)] for _ in range(2)] for _ in range(2)] for _ in range(2)]
            sct = [[[A.alloc(256) for _ in range(2)] for _ in range(2)] for _ in range(2)]
            Sb = [A.alloc(4 * 256, BF16).rearrange("p (d r n) -> p d r n", d=2, r=2) for _ in range(2)]
            glua = [A.alloc(512) for _ in range(2)]
            dcount["bufs"] = glua
            glug = [A.alloc(512) for _ in range(2)]
            brs_st = [A.alloc(512, BF16) for _ in range(2)]
            for s_ in range(2):
                for d in range(2):
                    for pp_ in range(2):
                        for ri in range(2):
                            S.op(PO, lambda e, b=scb[s_][d][pp_][ri]: e.memset(b, 0.0), writes=[("scb", s_, d, pp_, ri)])
            def Vpart(gpp):
                par = (gpp // 2) % 2
                w2(4)
                for ss in range(2):
                    gp = gpp + ss
                    vb = 2 * (1 - ss)

                    def vgen(e, gp=gp, vb=vb):
                        ins = None
                        for d in range(2):
                            for ri in range(2):
                                for kt in range(2):
                                    ins = mm(e, bank(vb + d, ri * 256, ri * 256 + 256), XVb[:, d, ri, gp, kt, :], Ug[:, gp, kt, :], kt == 0, kt == 1)
                        return ins
                    S.op(PE, vgen, reads=["XVb", ("Ug", gp)], writes=[("ps", vb), ("ps", vb + 1)])
                    for d in range(2):
                        for ri in range(2):
                            dst = scb[ss][d][par][ri][:, 128:384] if d == 0 else scb[ss][d][par][ri][:, 0:256]
                            copy_op(AC, dst, bank(vb + d, ri * 256, ri * 256 + 256), [("ps", vb + d)], [("scb", ss, d, par, ri)])

            def KSpart(gpp):
                cur = (gpp // 2) % 2
                for k in range(8):
                    sh_ = 1 << k
                    mi = MI(8 * sh_)
                    first, second = [], []
                    for ss in range(2):
                        gp = gpp + ss
                        for d in range(2):
                            ee = d * 16 + gp
                            ar = PWr[:, mi, ee:ee + 1]
                            ai = PWi[:, mi, ee:ee + 1]
                            nai = PWni[:, mi, ee:ee + 1]
                            srcb = scb[ss][d][cur]
                            dstb = scb[ss][d][1 - cur]
                            if d == 0:
                                dat = slice(128, 384)
                                shf = slice(128 - sh_, 384 - sh_)
                            else:
                                dat = slice(0, 256)
                                shf = slice(sh_, 256 + sh_)
                            kS = [("scb", ss, d, cur, 0), ("scb", ss, d, cur, 1)]
                            kD = [("scb", ss, d, 1 - cur, 0), ("scb", ss, d, 1 - cur, 1)]
                            tA, tB = sct[ss][d]
                            first.append((lambda e, srcb=srcb, ar=ar, shf=shf, dat=dat, tA=tA: e.scalar_tensor_tensor(out=tA, in0=srcb[0][:, shf], scalar=ar, in1=srcb[0][:, dat], op0=ALU.mult, op1=ALU.add), kS + ["PW"], [("sct", ss, d, 0)]))
                            first.append((lambda e, srcb=srcb, ar=ar, shf=shf, dat=dat, tB=tB: e.scalar_tensor_tensor(out=tB, in0=srcb[1][:, shf], scalar=ar, in1=srcb[1][:, dat], op0=ALU.mult, op1=ALU.add), kS, [("sct", ss, d, 1)]))
                            second.append((lambda e, srcb=srcb, dstb=dstb, nai=nai, shf=shf, dat=dat, tA=tA: e.scalar_tensor_tensor(out=dstb[0][:, dat], in0=srcb[1][:, shf], scalar=nai, in1=tA, op0=ALU.mult, op1=ALU.add), kS + [("sct", ss, d, 0)], [kD[0]]))
                            second.append((lambda e, srcb=srcb, dstb=dstb, ai=ai, shf=shf, dat=dat, tB=tB: e.scalar_tensor_tensor(out=dstb[1][:, dat], in0=srcb[0][:, shf], scalar=ai, in1=tB, op0=ALU.mult, op1=ALU.add), kS + [("sct", ss, d, 1)], [kD[1]]))
                    for (fn, rd, wr) in first + second:
                        S.op(VE, fn, reads=rd, writes=wr)
                    cur = 1 - cur

            def Tail(gpp):
                cur = (gpp // 2) % 2
                for ss in range(2):
                    gp = gpp + ss
                    for d in range(2):
                        for ri in range(2):
                            srcv = scb[ss][d][cur][ri][:, 127:383] if d == 0 else scb[ss][d][cur][ri][:, 1:257]
                            copy_op(AC, Sb[ss][:, d, ri, :], srcv, [("scb", ss, d, cur, ri)], [("Sb", ss, d, ri)])
                    q4 = gp % 4

                    def ygen(e, gp=gp, ss=ss, q4=q4):
                        ins = None
                        for sl in range(8):
                            o = pst[32 * q4:32 * q4 + 32, 2048 + sl * 256:2048 + sl * 256 + 256]
                            mm(e, o, Tp[:, gp, 0, sl, :], Ug[:, gp, 0, :], True, False, tp=(0, 32 * q4))
                            mm(e, o, Tp[:, gp, 1, sl, :], Ug[:, gp, 1, :], False, False, tp=(0, 32 * q4))
                            for d in range(2):
                                for ri in range(2):
                                    ins = mm(e, o, YCb[:, d, ri, gp, sl, :], Sb[ss][:, d, ri, :], False, (d == 1 and ri == 1), tp=(0, 32 * q4))
                        return ins
                    S.op(PE, ygen, reads=["Tp", "YCb", ("Ug", gp)] + [("Sb", ss, d, ri) for d in range(2) for ri in range(2)], writes=["psY"])
                    if q4 == 3:
                        fc = gp // 4
                        for sl in range(8):
                            S.op(AC, lambda e, fc=fc, sl=sl: e.activation(out=ygT[:, fc, sl::8], in_=pst[:, 2048 + sl * 256:2048 + sl * 256 + 256], func=AF.Gelu_apprx_tanh), reads=["psY"], writes=["ygT"])

            Vpart(0)
            for gpp in range(0, 16, 2):
                KSpart(gpp)
                if gpp + 2 < 16:
                    Vpart(gpp + 2)
                Tail(gpp)
            if debug and seq == 0:
                for fc in range(4):
                    for hh in range(4):
                        i2 = (fc * 4 + hh) % 2
                        S.op(VE, lambda e, fc=fc, hh=hh, i2=i2: e.tensor_copy(out=glua[i2], in_=ygT[:, fc, hh * 512:(hh + 1) * 512]), reads=["ygT"], writes=[("glua", i2)])
                        dma(SY, dbg["d_yg"][:, fc * 2048 + hh * 512: fc * 2048 + (hh + 1) * 512], glua[i2], [("glua", i2)], [], final=True)
            for tb in range(4):
                tsl = slice(tb * 512, (tb + 1) * 512)
                for j in range(4):
                    i2 = (tb * 4 + j) % 2

                    def glu_mm(e, j=j, tsl=tsl):
                        ins = None
                        for half, bk in ((0, 0), (1, 1)):
                            for kc in range(4):
                                ins = mm(e, bank(bk), wglu[:, kc, half * 512 + j * 128: half * 512 + j * 128 + 128], ygT[:, kc, tsl], kc == 0, kc == 3)
                        return ins
                    S.op(PE, glu_mm, reads=["wglu", "ygT"], writes=[("ps", 0), ("ps", 1)])
                    S.op(AC, lambda e, j=j, i2=i2: e.activation(out=glug[i2], in_=bank(1), func=AF.Sigmoid, bias=pp[:, PP_BGLU + 4 + j:PP_BGLU + 5 + j]), reads=[("ps", 1), "pp"], writes=[("glug", i2)])
                    S.op(VE, lambda e, j=j, i2=i2: e.scalar_tensor_tensor(out=brs_st[i2], in0=bank(0), scalar=pp[:, PP_BGLU + j:PP_BGLU + j + 1], in1=glug[i2], op0=ALU.add, op1=ALU.mult), reads=[("ps", 0), "pp", ("glug", i2)], writes=[("brs", i2)])
                    dma(SY, brssm_h[j * 128:(j + 1) * 128, seq * SEQ + tb * 512: seq * SEQ + (tb + 1) * 512], brs_st[i2], [("brs", i2)], [])
                    if debug:
                        S.op(VE, lambda e, i2=i2: e.tensor_copy(out=glua[i2], in_=brs_st[i2]), reads=[("brs", i2)], writes=[("glua", i2)])
                        dma(SY, dbg["d_ssm"][j * 128:(j + 1) * 128, seq * SEQ + tb * 512: seq * SEQ + (tb + 1) * 512], glua[i2], [("glua", i2)], [], final=True)
            S.barrier()
        w2_flush()
        S.barrier()
        A.release(mW0)
        if stop_after == "P1":
            S.emit()
            return nc

        brna = A.alloc(4 * SEQ, BF16).rearrange("p (f t) -> p f t", f=4)
        mSeq = A.mark()
        for seq in range(NSEQ):
            A.release(mSeq)
            nab = A.alloc(NPAT * 8 * 128, BF16).rearrange("p (a h q) -> p a h q", a=NPAT, h=8)
            xb = A.alloc(8 * SEQ, BF16).rearrange("p (k t) -> p k t", k=8)
            xst = [A.alloc(4 * 512).rearrange("p (k t) -> p k t", k=4) for _ in range(2)]
            ci = 0
            for c0 in range(0, NPAT * 8 * 128, 2048):
                c1 = min(c0 + 2048, NPAT * 8 * 128)
                i = ci % 2
                ci += 1
                nst = xst[i].rearrange("p k t -> p (k t)")
                dma(SY, nst[:, 0:c1 - c0], nab_in[:, c0:c1], [], [("xst", i)])
                copy_op(evac_eng(), nab.rearrange("p a h q -> p (a h q)")[:, c0:c1], nst[:, 0:c1 - c0], [("xst", i)], ["nab"])
            nabf = nab.rearrange("p a h q -> p (a h q)")
            for c0 in range(0, NPAT * 8 * 128, 4096):
                c1 = min(c0 + 4096, NPAT * 8 * 128)
                S.op(AC, lambda e, c0=c0, c1=c1: e.activation(out=nabf[:, c0:c1], in_=nabf[:, c0:c1], func=AF.Exp), reads=["nab"], writes=["nab"])
            for tb in range(4):
                for kc0 in range(0, 8, 4):
                    i = ci % 2
                    ci += 1
                    dma(SY, xst[i], xTv[:, kc0:kc0 + 4, seq * SEQ + tb * 512: seq * SEQ + (tb + 1) * 512], [], [("xst", i)])
                    copy_op(evac_eng(), xb[:, kc0:kc0 + 4, tb * 512:(tb + 1) * 512], xst[i], [("xst", i)], ["xb"])
            wq = A.alloc(8 * 512, BF16).rearrange("p (k n) -> p k n", k=8)
            qT = A.alloc(4 * SEQ, BF16).rearrange("p (f t) -> p f t", f=4)
            kT = A.alloc(4 * SEQ, BF16).rearrange("p (f t) -> p f t", f=4)
            Vt = A.alloc(16 * 512, BF16).rearrange("p (t n) -> p t n", t=16)
            pbk = 0
            for which, dst in ((1, qT), (2, kT)):
                dma(SY, wq, wb_in.rearrange("(k p) n -> p k n", p=128)[:, :, which * 512:(which + 1) * 512], [], ["wq"])
                for f in range(4):
                    for tb in range(4):
                        bk = pbk % 4
                        pbk += 1

                        def pmm(e, f=f, tb=tb, bk=bk):
                            ins = None
                            for kc in range(8):
                                ins = mm(e, bank(bk), wq[:, kc, f * 128:(f + 1) * 128], xb[:, kc, tb * 512:(tb + 1) * 512], kc == 0, kc == 7)
                            return ins
                        S.op(PE, pmm, reads=["wq", "xb"], writes=[("ps", bk)])
                        eng = evac_eng()
                        copy_op(eng, dst[:, f, tb * 512:(tb + 1) * 512], bank(bk), [("ps", bk)], ["qk%d" % which], scale=(0.125 if which == 1 else None))
            dma(SY, wq, wb_in.rearrange("(k p) n -> p k n", p=128)[:, :, 1536:2048], [], ["wq"])
            for tt in range(16):
                bk = pbk % 4
                pbk += 1

                def vmm(e, tt=tt, bk=bk):
                    ins = None
                    for kc in range(8):
                        ins = mm(e, bank(bk), xb[:, kc, tt * 128:(tt + 1) * 128], wq[:, kc, :], kc == 0, kc == 7)
                    return ins
                S.op(PE, vmm, reads=["wq", "xb"], writes=[("ps", bk)])
                eng = evac_eng()
                copy_op(eng, Vt[:, tt, :], bank(bk), [("ps", bk)], ["Vt"])
            Pm = [A.alloc(640, BF16) for _ in range(3)]
            rec = [A.alloc(128) for _ in range(2)]
            dna = [A.alloc(128) for _ in range(2)]
            ui = 0
            unitsA, unitsB = [], []
            for tt in range(16):
                lst = NA_PER_T[tt]
                nk = len(lst)
                for hc in range(4):
                    ob = 4 + ((tt * 4 + hc) % 2)
                    for hh in range(2):
                        h = 2 * hc + hh
                        sb = (0, 2, 6)[ui % 3]
                        pi_ = ui % 3
                        ui += 1
                        rows = slice(64 * hh, 64 * hh + 64)

                        def partA(lst=lst, nk=nk, h=h, hc=hc, rows=rows, sb=sb, tt=tt, pi_=pi_):
                            def smm(e):
                                ins = None
                                for j, (kt, pat) in enumerate(lst):
                                    o = pst[:, sb * 512 + j * 128: sb * 512 + j * 128 + 128]
                                    ins = mm(e, o, kT[rows, hc, kt * 128:(kt + 1) * 128], qT[rows, hc, tt * 128:(tt + 1) * 128], True, True)
                                return ins
                            S.op(PE, smm, reads=["qk1", "qk2"], writes=[("ps", sb), ("ps", sb + 1)])
                            n1 = min(nk, 4) * 128
                            S.op(AC, lambda e: e.activation(out=Pm[pi_][:, 0:n1], in_=pst[:, sb * 512: sb * 512 + n1], func=AF.Exp), reads=[("ps", sb)], writes=[("Pm", pi_)])
                            if nk > 4:
                                S.op(AC, lambda e: e.activation(out=Pm[pi_][:, 512:640], in_=pst[:, sb * 512 + 512: sb * 512 + 640], func=AF.Exp), reads=[("ps", sb + 1), ("Pm", pi_)], writes=[("Pm", pi_)])

                            def pmul(e):
                                ins = None
                                for j, (kt, pat) in enumerate(lst):
                                    ins = e.tensor_tensor(out=Pm[pi_][:, j * 128:(j + 1) * 128], in0=Pm[pi_][:, j * 128:(j + 1) * 128], in1=nab[:, pat, h, :], op=ALU.mult)
                                return ins
                            S.op(VE, pmul, reads=[("Pm", pi_), "nab"], writes=[("Pm", pi_)])

                        def partB(lst=lst, nk=nk, h=h, hh=hh, hc=hc, ob=ob, pi_=pi_, tt=tt):
                            def pvmm(e):
                                ins = None
                                for j, (kt, pat) in enumerate(lst):
                                    mm(e, pst[64 * hh:64 * hh + 64, ob * 512: ob * 512 + 128], Vt[:, kt, 64 * h:64 * h + 64], Pm[pi_][:, j * 128:(j + 1) * 128], j == 0, j == nk - 1, tp=(0, 64 * hh))
                                for j, (kt, pat) in enumerate(lst):
                                    ins = mm(e, pst[64 * hh:64 * hh + 64, ob * 512 + 128: ob * 512 + 256], onesb[:, 0:64], Pm[pi_][:, j * 128:(j + 1) * 128], j == 0, j == nk - 1, tp=(0, 64 * hh))
                                return ins
                            S.op(PE, pvmm, reads=["Vt", ("Pm", pi_), "onesb"], writes=[("ps", ob)])
                            if hh == 1:
                                oi = (tt * 4 + hc) % 2
                                S.op(VE, lambda e: e.reciprocal(out=rec[oi], in_=pst[:, ob * 512 + 128: ob * 512 + 256]), reads=[("ps", ob)], writes=[("rec", oi)])
                                S.op(VE, lambda e: e.tensor_tensor(out=brna[:, hc, tt * 128:(tt + 1) * 128], in0=pst[:, ob * 512: ob * 512 + 128], in1=rec[oi], op=ALU.mult), reads=[("ps", ob), ("rec", oi)], writes=["brna"])
                                if debug:
                                    S.op(VE, lambda e: e.tensor_copy(out=dna[oi], in_=brna[:, hc, tt * 128:(tt + 1) * 128]), reads=["brna"], writes=[("dna", oi)])
                                    dma(SY, dbg["d_na"][hc * 128:(hc + 1) * 128, seq * SEQ + tt * 128: seq * SEQ + (tt + 1) * 128], dna[oi], [("dna", oi)], [], final=True)
                        unitsA.append(partA)
                        unitsB.append(partB)
            LAG = 2
            for u in range(len(unitsA) + LAG):
                if u < len(unitsA):
                    unitsA[u]()
                if u - LAG >= 0:
                    unitsB[u - LAG]()
            S.barrier()
            A.release(mSeq)
            if stop_after == "P2":
                S.emit()
                return nc
            phase3(nc, S, A, pst, bank, seq, locals())
            S.barrier()
            if stop_after == "P3":
                S.emit()
                return nc
        S.emit()
    return nc


def phase3(nc, S, A, pst, bank, seq, env):
    SY, AC, PO, VE, PE = "sync", "scalar", "gpsimd", "vector", "tensor"
    debug = env["debug"]
    dbg = env["dbg"]
    pp = env["pp"]
    brna = env["brna"]
    onesb = env["onesb"]
    onesf = env["onesf"]
    mm = env["mm"]
    dma = env["dma"]
    copy_op = env["copy_op"]
    evac_eng = env["evac_eng"]
    xTv = env["xTv"]
    memT = env["memT"]
    yT = env["yT"]
    brssm_h = env["brssm_h"]
    wb_in, wb_gate, wb_br, wb_out, wb_up, wb_down, wb_kv = (env[k] for k in ["wb_in", "wb_gate", "wb_br", "wb_out", "wb_up", "wb_down", "wb_kv"])
    NB = 4
    t2f = A.alloc(8 * 512).rearrange("p (f t) -> p f t", f=8)
    mst = t2f.rearrange("p f t -> p (f t)")[:, 0:2048].rearrange("p (k m) -> p k m", k=8)
    memb = A.alloc(8 * 256, BF16).rearrange("p (k m) -> p k m", k=8)
    kmT = A.alloc(4 * 256, BF16).rearrange("p (h m) -> p h m", h=4)
    Vm = A.alloc(2 * 512, BF16).rearrange("p (t n) -> p t n", t=2)
    RS = 4608
    NR = 3
    ring = [A.alloc(RS, BF16) for _ in range(NR)]
    dma(SY, mst, memT.rearrange("(k p) m -> p k m", p=128)[:, :, seq * 256:(seq + 1) * 256], [], ["t2f"])
    copy_op(VE, memb, mst, ["t2f"], ["memb"])
    wkv = ring[0][:, 0:4096].rearrange("p (k n) -> p k n", k=8)
    for half in range(2):
        dma(SY, wkv, wb_kv.rearrange("(k p) n -> p k n", p=128)[:, :, half * 512:(half + 1) * 512], [], [("ring", 0)])
        if half == 0:
            for h in range(4):
                def kmm(e, h=h):
                    ins = None
                    for kc in range(8):
                        ins = mm(e, bank(h, 0, 256), wkv[:, kc, h * 128:(h + 1) * 128], memb[:, kc, :], kc == 0, kc == 7)
                    return ins
                S.op(PE, kmm, reads=[("ring", 0), "memb"], writes=[("ps", h)])
                copy_op(evac_eng(), kmT[:, h, :], bank(h, 0, 256), [("ps", h)], ["kmT"])
        else:
            for mt in range(2):
                def vmm2(e, mt=mt):
                    ins = None
                    for kc in range(8):
                        ins = mm(e, bank(4 + mt), memb[:, kc, mt * 128:(mt + 1) * 128], wkv[:, kc, :], kc == 0, kc == 7)
                    return ins
                S.op(PE, vmm2, reads=[("ring", 0), "memb"], writes=[("ps", 4 + mt)])
                copy_op(evac_eng(), Vm[:, mt, :], bank(4 + mt), [("ps", 4 + mt)], ["Vm"])

    xf = [A.alloc(512) for _ in range(2)]
    xstg = [A.alloc(512) for _ in range(2)]
    sq = [A.alloc(512) for _ in range(2)]
    st_mean, st_var, st_rstd, st_nmr, st_tmp = [A.alloc(512) for _ in range(5)]
    X1 = A.alloc(8 * 513).rearrange("p (f t) -> p f t", f=8)
    XB = A.alloc(8 * 514, BF16).rearrange("p (f t) -> p f t", f=8)
    dtmp = None
    xblk = A.alloc(8 * 512, BF16).rearrange("p (k t) -> p k t", k=8)
    qm = A.alloc(4 * 512, BF16).rearrange("p (h t) -> p h t", h=4)
    Pmem2 = [A.alloc(2 * 512, BF16).rearrange("p (m t) -> p m t", m=2) for _ in range(2)]
    brm = A.alloc(4 * 512, BF16).rearrange("p (h t) -> p h t", h=4)
    brs = A.alloc(4 * 512, BF16).rearrange("p (h t) -> p h t", h=4)
    gat = A.alloc(3 * 512)
    gatv = gat.rearrange("p (n t) -> p n t", n=3)
    prd = A.alloc(3 * 512).rearrange("p (n t) -> p n t", n=3)
    gsum = A.alloc(8 * 512, BF16).rearrange("p (f t) -> p f t", f=8)
    recm2 = [A.alloc(512) for _ in range(2)]
    hid = A.alloc(22 * 512, BF16).rearrange("p (f t) -> p f t", f=22)
    accs = [(A.alloc(512), A.alloc(512), A.alloc(512)) for _ in range(2)]
    if seq == 0:
        print("[arena] phase3 top", A.off, "cap", A.cap)
    S.op(PO, lambda e: e.memset(XB.rearrange("p f t -> p (f t)"), 0.0), writes=["xbh"] + [("xbd", f) for f in range(8)])
    S.op(PO, lambda e: e.memset(X1.rearrange("p f t -> p (f t)"), 0.0), writes=["x1h"] + [("x1d", f) for f in range(8)])

    lnq = []

    def ln_chunk_stats(tag, tsrc_f, f, N, lag=1):
        i = f % 2
        src = tsrc_f(f)
        S.op(PO, lambda e, i=i, src=src: e.tensor_tensor(out=sq[i][:, 0:N], in0=src, in1=src, op=ALU.mult), reads=[("t", tag, f)], writes=[("sq", i)])

        def stmm(e, f=f, i=i, src=src):
            mm(e, bank(6, 0, N), onesf, src, f == 0, f == 7)
            return mm(e, bank(7, 0, N), onesf, sq[i][:, 0:N], f == 0, f == 7)
        lnq.append((stmm, [("sq", i), "onesf", ("t", tag, f)]))
        while len(lnq) > (lag if f < 7 else 0):
            fn_, rd_ = lnq.pop(0)
            S.op(PE, fn_, reads=rd_, writes=[("ps", 6), ("ps", 7)])

    def ln_apply(tag, tsrc_f, N, gcol, bcol, post):
        S.op(AC, lambda e: e.activation(out=st_mean[:, 0:N], in_=bank(6, 0, N), func=AF.Copy, scale=1.0 / D), reads=[("ps", 6)], writes=["st_mean"])
        S.op(VE, lambda e: e.tensor_tensor(out=st_tmp[:, 0:N], in0=st_mean[:, 0:N], in1=st_mean[:, 0:N], op=ALU.mult), reads=["st_mean"], writes=["st_tmp"])
        S.op(VE, lambda e: e.scalar_tensor_tensor(out=st_var[:, 0:N], in0=bank(7, 0, N), scalar=1.0 / D, in1=st_tmp[:, 0:N], op0=ALU.mult, op1=ALU.subtract), reads=[("ps", 7), "st_tmp"], writes=["st_var"])
        S.op(VE, lambda e: e.tensor_scalar(out=st_var[:, 0:N], in0=st_var[:, 0:N], scalar1=float(EPS), scalar2=None, op0=ALU.add), reads=["st_var"], writes=["st_var"])
        S.op(AC, lambda e: e.activation(out=st_tmp[:, 0:N], in_=st_var[:, 0:N], func=AF.Sqrt), reads=["st_var", "st_tmp"], writes=["st_tmp"])
        S.op(VE, lambda e: e.reciprocal(out=st_rstd[:, 0:N], in_=st_tmp[:, 0:N]), reads=["st_tmp"], writes=["st_rstd"])
        S.op(VE, lambda e: e.scalar_tensor_tensor(out=st_nmr[:, 0:N], in0=st_mean[:, 0:N], scalar=-1.0, in1=st_rstd[:, 0:N], op0=ALU.mult, op1=ALU.mult), reads=["st_mean", "st_rstd"], writes=["st_nmr"])
        for f in range(8):
            kf = ("t", tag, f)
            dst = tsrc_f(f)
            S.op(VE, lambda e, f=f, dst=dst: e.scalar_tensor_tensor(out=dst, in0=dst, scalar=pp[:, gcol + f:gcol + f + 1], in1=st_rstd[:, 0:N], op0=ALU.mult, op1=ALU.mult), reads=[kf, "st_rstd", "pp"], writes=[kf])
            S.op(VE, lambda e, f=f, dst=dst: e.scalar_tensor_tensor(out=dst, in0=st_nmr[:, 0:N], scalar=pp[:, gcol + f:gcol + f + 1], in1=dst, op0=ALU.mult, op1=ALU.add), reads=[kf, "st_nmr", "pp"], writes=[kf])
            S.op(AC, lambda e, f=f, dst=dst: e.activation(out=dst, in_=dst, func=AF.Identity, bias=pp[:, bcol + f:bcol + f + 1]), reads=[kf, "pp"], writes=[kf])
            post(f, kf)

    pieces = []
    rstate = {"loaded": 0}

    def flush(upto):
        while rstate["loaded"] < min(upto, len(pieces)):
            i = rstate["loaded"]
            pieces[i](ring[i % NR], ("ring", i % NR))
            rstate["loaded"] += 1

    def mixer_steps(b):
        steps = []
        t0 = seq * SEQ + b * 512
        tl0 = b * 512

        def s_load(_s, _k):
            for kc in range(8):
                i = kc % 2
                dma(SY, xstg[i], xTv[:, kc, t0:t0 + 512], [], [("xstg", i)])
                copy_op(VE if kc % 2 else AC, xblk[:, kc, :], xstg[i], [("xstg", i)], [("xblk", kc)])
            dma(SY, brs, brssm_h.rearrange("(f p) t -> p f t", p=128)[:, :, t0:t0 + 512], [], ["brs"])
        steps.append((None, s_load))
        xk = [("xblk", kc) for kc in range(8)]

        def p_qm(slot, key):
            dma(SY, slot[:, 0:4096].rearrange("p (k n) -> p k n", k=8), wb_in.rearrange("(k p) n -> p k n", p=128)[:, :, 2048:2560], [], [key])

        def s_qm(slot, key):
            w = slot[:, 0:4096].rearrange("p (k n) -> p k n", k=8)
            for h in range(4):
                bk = h % 2

                def qmm(e, h=h, bk=bk):
                    ins = None
                    for kc in range(8):
                        ins = mm(e, bank(bk), w[:, kc, h * 128:(h + 1) * 128], xblk[:, kc, :], kc == 0, kc == 7)
                    return ins
                S.op(PE, qmm, reads=[key] + xk, writes=[("ps", bk)])
                copy_op(evac_eng(), qm[:, h, :], bank(bk), [("ps", bk)], [("qm", h)])
            for h in range(4):
                hb = h % 2
                sb0 = 2 + 2 * hb
                ob0 = 0 if hb == 0 else 4
                Pmem = Pmem2[hb]
                recm = recm2[hb]

                def sc(e, h=h, sb0=sb0):
                    ins = None
                    for mt in range(2):
                        ins = mm(e, bank(sb0 + mt), kmT[:, h, mt * 128:(mt + 1) * 128], qm[:, h, :], True, True)
                    return ins
                S.op(PE, sc, reads=["kmT", ("qm", h)], writes=[("ps", sb0), ("ps", sb0 + 1)])
                for mt in range(2):
                    S.op(AC, lambda e, mt=mt, sb0=sb0, Pmem=Pmem: e.activation(out=Pmem[:, mt, :], in_=bank(sb0 + mt), func=AF.Exp, scale=128.0 ** -0.5), reads=[("ps", sb0 + mt)], writes=[("Pmem", hb, mt)])

                def pv(e, h=h, ob0=ob0, Pmem=Pmem):
                    ins = None
                    for mt in range(2):
                        mm(e, bank(ob0), Vm[:, mt, h * 128:(h + 1) * 128], Pmem[:, mt, :], mt == 0, mt == 1)
                    for mt in range(2):
                        ins = mm(e, bank(ob0 + 1), onesb, Pmem[:, mt, :], mt == 0, mt == 1)
                    return ins
                S.op(PE, pv, reads=["Vm", ("Pmem", hb, 0), ("Pmem", hb, 1), "onesb"], writes=[("ps", ob0), ("ps", ob0 + 1)])
                S.op(VE, lambda e, ob0=ob0, recm=recm: e.reciprocal(out=recm, in_=bank(ob0 + 1)), reads=[("ps", ob0 + 1)], writes=[("recm", hb)])
                S.op(VE, lambda e, h=h, ob0=ob0, recm=recm: e.tensor_tensor(out=brm[:, h, :], in0=bank(ob0), in1=recm, op=ALU.mult), reads=[("ps", ob0), ("recm", hb)], writes=[("brm", h)])
        steps.append((p_qm, s_qm))

        for f in range(8):
            def p_gb(slot, key, f=f):
                wg = slot[:, 0:3072].rearrange("p (k n m) -> p k n m", k=8, n=3)
                gsrc = wb_gate.rearrange("(k p) n -> p k n", p=128)
                for n in range(3):
                    dma(SY, wg[:, :, n, :], gsrc[:, :, n * 1024 + f * 128:n * 1024 + (f + 1) * 128], [], [key])
                wbv = slot[:, 3072:4608].rearrange("p (nk m) -> p nk m", nk=12)
                dma(SY, wbv, wb_br.rearrange("(nk p) m -> p nk m", p=128)[:, :, f * 128:(f + 1) * 128], [], [key])

            def s_gb(slot, key, f=f):
                wg = slot[:, 0:3072].rearrange("p (k n m) -> p k n m", k=8, n=3)
                wbv = slot[:, 3072:4608].rearrange("p (nk m) -> p nk m", nk=12)
                srcs = [brs, brna[:, :, tl0:tl0 + 512], brm]
                skeys = [["brs"], ["brna"], [("brm", h) for h in range(4)]]
                for n in range(3):
                    u_ = (f * 3 + n) % 3
                    bg, bb = 2 * u_, 2 * u_ + 1

                    def gbmm(e, n=n, bg=bg, bb=bb):
                        ins = None
                        for kc in range(8):
                            ins = mm(e, bank(bg), wg[:, kc, n, :], xblk[:, kc, :], kc == 0, kc == 7)
                        for kc in range(4):
                            ins = mm(e, bank(bb), wbv[:, n * 4 + kc, :], srcs[n][:, kc, :], kc == 0, kc == 3)
                        return ins
                    S.op(PE, gbmm, reads=[key] + xk + skeys[n], writes=[("ps", bg), ("ps", bb)])
                    S.op(AC, lambda e, n=n, bg=bg: e.activation(out=gatv[:, n, :], in_=bank(bg), func=AF.Sigmoid, bias=pp[:, PP_BG + n * 8 + f:PP_BG + n * 8 + f + 1]), reads=[("ps", bg), "pp"], writes=[("gat", n)])
                    S.op(VE, lambda e, n=n, bb=bb: e.tensor_tensor(out=prd[:, n, :], in0=bank(bb), in1=gatv[:, n, :], op=ALU.mult), reads=[("ps", bb), ("gat", n)], writes=[("prd", n)])
                S.op(PO, lambda e: e.tensor_tensor(out=prd[:, 0, :], in0=prd[:, 0, :], in1=prd[:, 1, :], op=ALU.add), reads=[("prd", 0), ("prd", 1)], writes=[("prd", 0)])
                S.op(VE, lambda e: e.tensor_tensor(out=gsum[:, f, :], in0=prd[:, 0, :], in1=prd[:, 2, :], op=ALU.add), reads=[("prd", 0), ("prd", 2)], writes=[("gsum", f)])
            steps.append((p_gb, s_gb))

        for half in range(2):
            def p_wo(slot, key, half=half):
                dma(SY, slot[:, 0:4096].rearrange("p (k n) -> p k n", k=8), wb_out.rearrange("(k p) n -> p k n", p=128)[:, :, half * 512:(half + 1) * 512], [], [key])

            def s_wo(slot, key, half=half):
                w = slot[:, 0:4096].rearrange("p (k n) -> p k n", k=8)
                if half == 0:
                    S.op(PO, lambda e: e.tensor_copy(out=XB[:, :, 0:2], in_=XB[:, :, 512:514]), reads=[("xbd", f) for f in range(8)], writes=["xbh"])
                    S.op(PO, lambda e: e.tensor_copy(out=X1[:, :, 0:1], in_=X1[:, :, 512:513]), reads=[("x1d", f) for f in range(8)], writes=["x1h"])
                for j in range(4):
                    f = half * 4 + j
                    bk = f % 4
                    xi_ = f % 2
                    dma(AC, xf[xi_], xTv[:, f, t0:t0 + 512], [], [("xf", xi_)])

                    def omm(e, j=j, bk=bk):
                        ins = None
                        for kc in range(8):
                            ins = mm(e, bank(bk), w[:, kc, j * 128:(j + 1) * 128], gsum[:, kc, :], kc == 0, kc == 7)
                        return ins
                    S.op(PE, omm, reads=[key] + [("gsum", kc) for kc in range(8)], writes=[("ps", bk)])
                    S.op(VE, lambda e, f=f, bk=bk, xi_=xi_: e.scalar_tensor_tensor(out=X1[:, f, 1:513], in0=xf[xi_], scalar=float(ALPHA), in1=bank(bk), op0=ALU.mult, op1=ALU.add), reads=[("xf", xi_), ("ps", bk), "x1h"], writes=[("x1d", f), ("t", ("m", b), f)])
                    ln_chunk_stats(("m", b), lambda ff: X1[:, ff, 1:513], f, 512)
                if half == 1:
                    def post(f, kf):
                        S.op(AC, lambda e, f=f: e.activation(out=XB[:, f, 2:514], in_=X1[:, f, 1:513], func=AF.Copy), reads=[kf, "xbh"], writes=[("xbd", f), ("x1d", f)])
                    ln_apply(("m", b), lambda ff: X1[:, ff, 1:513], 512, PP_L1G, PP_L1B, post)
                    if debug:
                        dma(SY, dbg["d_x1"].rearrange("(f p) t -> p f t", p=128)[:, :, t0:t0 + 512], X1[:, :, 1:513], [("x1d", f) for f in range(8)], [], final=True)
            steps.append((p_wo, s_wo))
        return steps

    def ffn_steps(w, N, tok_override=None):
        steps = []
        tok0 = seq * SEQ + 512 * w - 1 if tok_override is None else seq * SEQ + tok_override
        xbk = ["xbh"] + [("xbd", f) for f in range(8)]
        for pr in range(0, 22, 2):
            def p_up(slot, key, pr=pr):
                wv = slot[:, 0:4096].rearrange("p (k a m) -> p k a m", k=8, a=4)
                up = wb_up.rearrange("(k p) n -> p k n", p=128)
                dma(SY, wv[:, :, 0:2, :], up[:, :, pr * 128:(pr + 2) * 128].rearrange("p k (a m) -> p k a m", a=2), [], [key])
                dma(SY, wv[:, :, 2:4, :], up[:, :, DFF + pr * 128:DFF + (pr + 2) * 128].rearrange("p k (a m) -> p k a m", a=2), [], [key])

            def s_up(slot, key, pr=pr):
                wv = slot[:, 0:4096].rearrange("p (k a m) -> p k a m", k=8, a=4)
                for a in range(2):
                    j = pr + a
                    bs = 3 * (j % 2)
                    accg, accv, gl = accs[j % 2]
                    sx = j % 2

                    def umm(e, a=a, bs=bs):
                        ins = None
                        for which in range(2):
                            for kc in range(8):
                                ins = mm(e, bank(bs + which, 0, N), wv[:, kc, which * 2 + a, :], XB[:, kc, 1:1 + N], kc == 0, kc == 7)
                        for which in range(2):
                            for kc in range(8):
                                ins = mm(e, bank(bs + 2, which * 2, which * 2 + 2), wv[:, kc, which * 2 + a, :], XB[:, kc, 0:2 + N:1 + N], kc == 0, kc == 7)
                        return ins
                    S.op(PE, umm, reads=[key] + xbk, writes=[("ps", bs), ("ps", bs + 1), ("ps", bs + 2)])
                    for which, acc in ((0, accg), (1, accv)):
                        jj = j + 22 * which
                        w0 = pp[:, PP_CW + 0 * 44 + jj:PP_CW + 0 * 44 + jj + 1]
                        w1 = pp[:, PP_CW + 1 * 44 + jj:PP_CW + 1 * 44 + jj + 1]
                        w2 = pp[:, PP_CW + 2 * 44 + jj:PP_CW + 2 * 44 + jj + 1]
                        cb = pp[:, PP_CB + jj:PP_CB + jj + 1]
                        ka = ("acc", sx, which)
                        pb = bs + which
                        S.op(AC, lambda e, acc=acc, pb=pb, w1=w1, cb=cb: e.activation(out=acc[:, 0:N], in_=bank(pb, 0, N), func=AF.Identity, scale=w1, bias=cb), reads=[("ps", pb), "pp"], writes=[ka])
                        if N > 1:
                            S.op(VE, lambda e, acc=acc, pb=pb, w0=w0: e.scalar_tensor_tensor(out=acc[:, 1:N], in0=bank(pb, 0, N - 1), scalar=w0, in1=acc[:, 1:N], op0=ALU.mult, op1=ALU.add), reads=[("ps", pb), ka, "pp"], writes=[ka])
                            S.op(VE, lambda e, acc=acc, pb=pb, w2=w2: e.scalar_tensor_tensor(out=acc[:, 0:N - 1], in0=bank(pb, 1, N), scalar=w2, in1=acc[:, 0:N - 1], op0=ALU.mult, op1=ALU.add), reads=[("ps", pb), ka, "pp"], writes=[ka])
                        S.op(VE, lambda e, acc=acc, which=which, w0=w0, bs=bs: e.scalar_tensor_tensor(out=acc[:, 0:1], in0=bank(bs + 2, which * 2, which * 2 + 1), scalar=w0, in1=acc[:, 0:1], op0=ALU.mult, op1=ALU.add), reads=[("ps", bs + 2), ka, "pp"], writes=[ka])
                        S.op(VE, lambda e, acc=acc, which=which, w2=w2, bs=bs: e.scalar_tensor_tensor(out=acc[:, N - 1:N], in0=bank(bs + 2, which * 2 + 1, which * 2 + 2), scalar=w2, in1=acc[:, N - 1:N], op0=ALU.mult, op1=ALU.add), reads=[("ps", bs + 2), ka, "pp"], writes=[ka])
                    S.op(AC, lambda e, accg=accg, gl=gl: e.activation(out=gl[:, 0:N], in_=accg[:, 0:N], func=AF.Gelu_apprx_tanh), reads=[("acc", sx, 0)], writes=[("gl", sx)])
                    S.op(PO, lambda e, j=j, gl=gl, accv=accv: e.tensor_tensor(out=hid[:, j, 0:N], in0=gl[:, 0:N], in1=accv[:, 0:N], op=ALU.mult), reads=[("gl", sx), ("acc", sx, 1)], writes=[("hid", j)])
            steps.append((p_up, s_up))
        for f in range(8):
            def p_dn(slot, key, f=f):
                dma(SY, slot[:, 0:2816].rearrange("p (k m) -> p k m", k=22), wb_down.rearrange("(k p) n -> p k n", p=128)[:, :, f * 128:(f + 1) * 128], [], [key])

            def s_dn(slot, key, f=f):
                wv = slot[:, 0:2816].rearrange("p (k m) -> p k m", k=22)
                bk = f % 6

                def dmm_a(e, bk=bk):
                    ins = None
                    for kc in range(18):
                        ins = mm(e, bank(bk, 0, N), wv[:, kc, :], hid[:, kc, 0:N], kc == 0, False)
                    return ins

                def dmm_b(e, bk=bk):
                    ins = None
                    for kc in range(18, 22):
                        ins = mm(e, bank(bk, 0, N), wv[:, kc, :], hid[:, kc, 0:N], False, kc == 21)
                    return ins
                S.op(PE, dmm_a, reads=[key] + [("hid", j) for j in range(18)], writes=[("ps", bk)])
                S.op(PE, dmm_b, reads=[key] + [("hid", j) for j in range(18, 22)], writes=[("ps", bk)])
                S.op(VE, lambda e, bk=bk: e.scalar_tensor_tensor(out=t2f[:, f, 0:N], in0=X1[:, f, 0:N], scalar=float(ALPHA), in1=bank(bk, 0, N), op0=ALU.mult, op1=ALU.add), reads=[("x1d", f), "x1h", ("ps", bk)], writes=[("t", ("f", w), f), "t2f"])
                ln_chunk_stats(("f", w), lambda ff: t2f[:, ff, 0:N], f, N)
                if f == 7:
                    c0 = 1 if w == 0 else 0

                    def post(ff, kf):
                        dma(AC, yT[ff * 128:(ff + 1) * 128, tok0 + c0:tok0 + N], t2f[:, ff, c0:N], [kf, "t2f"], [], final=True)
                    ln_apply(("f", w), lambda ff: t2f[:, ff, 0:N], N, PP_L2G, PP_L2B, post)
            steps.append((p_dn, s_dn))
        return steps

    def final_shift_steps():
        def s_fin(_s, _k):
            S.op(PO, lambda e: e.tensor_copy(out=XB[:, :, 0:2], in_=XB[:, :, 511:513]), reads=[("xbd", f) for f in range(8)], writes=["xbh"])
            S.op(PO, lambda e: e.tensor_copy(out=XB[:, :, 2:3], in_=XB[:, :, 513:514]), reads=["xbh"] + [("xbd", f) for f in range(8)], writes=[("xbd", f) for f in range(8)])
            S.op(PO, lambda e: e.memset(XB[:, :, 3:4], 0.0), reads=["xbh"], writes=[("xbd", f) for f in range(8)])
            S.op(PO, lambda e: e.tensor_copy(out=X1[:, :, 0:1], in_=X1[:, :, 511:512]), reads=[("x1d", f) for f in range(8)], writes=["x1h"])
            S.op(PO, lambda e: e.tensor_copy(out=X1[:, :, 1:2], in_=X1[:, :, 512:513]), reads=["x1h"] + [("x1d", f) for f in range(8)], writes=[("x1d", f) for f in range(8)])
        return [(None, s_fin)]

    M = [mixer_steps(b) for b in range(NB)]
    F = [ffn_steps(w, 512) for w in range(NB)] + [final_shift_steps() + ffn_steps(NB, 2, tok_override=SEQ - 2)]
    order = list(M[0][:-2]) + (M[1][:1] if NB > 1 else []) + list(M[0][-2:])
    for w in range(NB):
        nxt = M[w + 1] if w + 1 < NB else []
        early, late = nxt[1:-2], nxt[-2:]
        if w + 2 < NB:
            late = late[:0] + [M[w + 2][0]] + late
        head, rest = early[:5], early[5:]
        order += head
        fs = F[w]
        k = 0
        for i, stp in enumerate(fs):
            order.append(stp)
            while k < len(rest) and (k + 1) * len(fs) <= (i + 1) * len(rest):
                order.append(rest[k])
                k += 1
        order += rest[k:]
        order += late
    order += F[NB]
    plan = []
    for (pc, fn) in order:
        if pc is not None:
            pieces.append(pc)
            plan.append((len(pieces) - 1, fn))
        else:
            plan.append((None, fn))
    for (pi, fn) in plan:
        if pi is None:
            fn(None, None)
        else:
            flush(pi + NR - 1)
            fn(ring[pi % NR], ("ring", pi % NR))


_CACHE = {}


def kernel(**inputs):
    f32 = np.float32
    x = np.asarray(inputs["x"], f32)
    mem = np.asarray(inputs["mem"], f32)
    sp = host_ssm_params(*(np.asarray(inputs[k], f32) for k in ["ssm_lambda_re", "ssm_lambda_im", "ssm_log_dt", "ssm_b_re", "ssm_b_im", "ssm_c_re", "ssm_c_im", "ssm_d"]))
    pp = host_small_params(*(np.asarray(inputs[k], f32) for k in ["b_gate", "b_glu", "ln1_g", "ln1_b", "ln2_g", "ln2_b", "conv_w", "conv_b"]))
    cst = host_consts()
    nab = host_na_bias(np.asarray(inputs["na_rpb"], f32))
    shared = {
        "w_in": np.ascontiguousarray(inputs["w_in"], f32), "w_gate": np.ascontiguousarray(inputs["w_gate"], f32),
        "w_glu": np.ascontiguousarray(inputs["w_glu"], f32), "w_mem_kv": np.ascontiguousarray(inputs["w_mem_kv"], f32),
        "w_branch": np.ascontiguousarray(np.asarray(inputs["w_branch"], f32).reshape(1536, D)),
        "w_out": np.ascontiguousarray(inputs["w_out"], f32), "w_up": np.ascontiguousarray(inputs["w_up"], f32),
        "w_down": np.ascontiguousarray(inputs["w_down"], f32), "sp": sp, "pp": pp, "cst": cst, "nab": nab,
    }
    in_maps = []
    for c in range(NCORES):
        xs = x[NSEQ * c:NSEQ * (c + 1)].reshape(TOK, D)
        ms = mem[NSEQ * c:NSEQ * (c + 1)].reshape(NSEQ * MEM, D)
        m = dict(shared)
        m["xT"] = np.ascontiguousarray(xs.T)
        m["memT"] = np.ascontiguousarray(ms.T)
        in_maps.append(m)
    if "nc" not in _CACHE:
        _CACHE["nc"] = build(False)
    res = run_bass_kernel_spmd(_CACHE["nc"], in_maps, core_ids=list(range(NCORES)))
    out = np.empty((NCORES * NSEQ, SEQ, D), f32)
    for c in range(NCORES):
        yT = np.asarray(res.results[c]["yT"], f32)
        out[NSEQ * c:NSEQ * (c + 1)] = yT.T.reshape(NSEQ, SEQ, D)
    return out
```
